# Optimizing a Trainium2 kernel written in Bass

```python
import math
import jax, jax.numpy as jnp
from jax import lax
import numpy as np

D_MODEL = 1024
BATCH = 8
SEQ = 2048
DEPTH = 2

CHUNK = 128
A_GROUPS = 4
A_GROUP_DIM = 128
A_WIDTH = A_GROUPS * A_GROUP_DIM
B_HEADS = 4
B_HEAD_DIM = 64
B_VDIM = 2 * B_HEAD_DIM
B_WIDTH = B_HEADS * B_VDIM
Q_BLOCK = 128
C_WINDOWS = (2, 4, 8, 16)
C_GROUPS = len(C_WINDOWS)
C_GROUP_DIM = 128
C_WIDTH = C_GROUPS * C_GROUP_DIM
N_BRANCH = 3
BRANCH_WIDTH = 512
IN_A = 2 * A_WIDTH
IN_Q = B_HEADS * 2 * B_HEAD_DIM
IN_K = B_HEADS * 2 * B_HEAD_DIM
IN_V = B_WIDTH
IN_C = C_WIDTH
IN_G = N_BRANCH * D_MODEL
IN_TOTAL = IN_A + IN_Q + IN_K + IN_V + IN_C + IN_G
SPLITS = tuple(int(s) for s in np.cumsum([IN_A, IN_Q, IN_K, IN_V, IN_C]))
D_FF = -(-8 * D_MODEL // (3 * 256)) * 256
EPS = 1e-6

kernel_name = "hybrid_gated_gmlp_diffattn_pool_block"


def rmsnorm(x, g):
    xf = x.astype(jnp.float32)
    y = xf * lax.rsqrt(jnp.mean(xf * xf, axis=-1, keepdims=True) + EPS)
    return (y * g.astype(jnp.float32)).astype(x.dtype)


def layernorm(x, g, b):
    xf = x.astype(jnp.float32)
    mu = jnp.mean(xf, axis=-1, keepdims=True)
    var = jnp.mean(jnp.square(xf - mu), axis=-1, keepdims=True)
    y = (xf - mu) * lax.rsqrt(var + EPS)
    return (y * g.astype(jnp.float32) + b.astype(jnp.float32)).astype(x.dtype)


def lambda_init_fn(layer_idx):
    return 0.8 - 0.6 * math.exp(-0.3 * layer_idx)


def gmlp_branch(za, vn_g, vn_b, w_s, b_s):
    bsz, s, _ = za.shape
    u, v = jnp.split(za, 2, axis=-1)
    v = layernorm(v, vn_g, vn_b)
    vc = v.reshape(bsz, s // CHUNK, CHUNK, A_GROUPS, A_GROUP_DIM)
    causal = jnp.tril(jnp.ones((CHUNK, CHUNK), dtype=bool))
    ws = jnp.where(causal[None], w_s, jnp.zeros_like(w_s))
    mixed = jnp.einsum('gts,bcsgd->bctgd', ws, vc) + b_s.T[None, None, :, :, None]
    return u * mixed.reshape(bsz, s, A_WIDTH)


def diff_attention(q, k, v, lq1, lk1, lq2, lk2, subln_g, lambda_init):
    bsz, s, _ = q.shape
    q = q.reshape(bsz, s, B_HEADS, 2, B_HEAD_DIM)
    k = k.reshape(bsz, s, B_HEADS, 2, B_HEAD_DIM)
    v = v.reshape(bsz, s, B_HEADS, B_VDIM)
    f32 = jnp.float32
    lam = (jnp.exp(jnp.sum(lq1.astype(f32) * lk1.astype(f32)))
           - jnp.exp(jnp.sum(lq2.astype(f32) * lk2.astype(f32))) + lambda_init)
    scale = B_HEAD_DIM ** -0.5
    nb = s // Q_BLOCK
    qb = jnp.moveaxis(q.reshape(bsz, nb, Q_BLOCK, B_HEADS, 2, B_HEAD_DIM), 1, 0)
    kpos = jnp.arange(s)

    def block(args):
        qblk, i = args
        sc = jnp.einsum('bqhcd,bkhcd->bhcqk', qblk, k).astype(f32) * scale
        qpos = i * Q_BLOCK + jnp.arange(Q_BLOCK)
        mask = kpos[None, :] <= qpos[:, None]
        sc = jnp.where(mask, sc, -jnp.inf)
        p = jax.nn.softmax(sc, axis=-1)
        w = p[:, :, 0] - lam * p[:, :, 1]
        return jnp.einsum('bhqk,bkhe->bqhe', w.astype(v.dtype), v)

    out = lax.map(block, (qb, jnp.arange(nb)))
    out = jnp.moveaxis(out, 0, 1).reshape(bsz, s, B_HEADS, B_VDIM)
    out = rmsnorm(out, subln_g) * (1.0 - lambda_init)
    return out.reshape(bsz, s, B_WIDTH)


def pool_branch(p, w_pool, pool_scale):
    bsz, s, _ = p.shape
    pf = p.astype(jnp.float32)
    csum = jnp.concatenate([jnp.zeros((bsz, 1, C_WIDTH), jnp.float32),
                            lax.cumsum(pf, axis=1)], axis=1)
    t = jnp.arange(s)
    outs = []
    for g, w in enumerate(C_WINDOWS):
        sl = slice(g * C_GROUP_DIM, (g + 1) * C_GROUP_DIM)
        c = csum[..., sl]
        lo = jnp.take(c, jnp.maximum(t + 1 - w, 0), axis=1)
        cnt = jnp.minimum(t + 1, w).astype(jnp.float32)
        outs.append((c[:, 1:] - lo) / cnt[None, :, None] - pf[..., sl])
    pooled = jnp.stack(outs, axis=2).astype(p.dtype)
    y = jnp.einsum('bsgc,gcd->bsgd', pooled, w_pool).reshape(bsz, s, C_WIDTH)
    return y * pool_scale


def setup_inputs(seed: int = 0) -> dict:
    key = jax.random.key(seed)
    ks = jax.random.split(key, 24)
    f32 = jnp.float32
    nrm = lambda k, shape, scale: jax.random.normal(k, shape, f32) * scale
    gain = lambda k, shape: 1.0 + 0.05 * jax.random.normal(k, shape, f32)
    L = DEPTH
    return {
        "x": jax.random.normal(ks[0], (BATCH, SEQ, D_MODEL), f32),
        "norm_mix_pre": gain(ks[1], (L, D_MODEL)),
        "w_in": nrm(ks[2], (L, D_MODEL, IN_TOTAL), D_MODEL ** -0.5),
        "gmlp_norm_g": gain(ks[3], (L, A_WIDTH)),
        "gmlp_norm_b": nrm(ks[4], (L, A_WIDTH), 0.02),
        "gmlp_w_s": nrm(ks[5], (L, A_GROUPS, CHUNK, CHUNK), CHUNK ** -0.5),
        "gmlp_b_s": gain(ks[6], (L, A_GROUPS, CHUNK)),
        "lambda_q1": nrm(ks[7], (L, B_HEAD_DIM), 0.1),
        "lambda_k1": nrm(ks[8], (L, B_HEAD_DIM), 0.1),
        "lambda_q2": nrm(ks[9], (L, B_HEAD_DIM), 0.1),
        "lambda_k2": nrm(ks[10], (L, B_HEAD_DIM), 0.1),
        "diff_subln_g": gain(ks[11], (L, B_VDIM)),
        "pool_w": nrm(ks[12], (L, C_GROUPS, C_GROUP_DIM, C_GROUP_DIM), C_GROUP_DIM ** -0.5),
        "pool_scale": gain(ks[13], (L, C_WIDTH)),
        "w_branch": nrm(ks[14], (L, N_BRANCH, BRANCH_WIDTH, D_MODEL), BRANCH_WIDTH ** -0.5),
        "w_out": nrm(ks[15], (L, D_MODEL, D_MODEL), D_MODEL ** -0.5),
        "norm_mix_post": gain(ks[16], (L, D_MODEL)),
        "norm_ffn_pre": gain(ks[17], (L, D_MODEL)),
        "w_ffn_in": nrm(ks[18], (L, D_MODEL, 2 * D_FF), D_MODEL ** -0.5),
        "w_ffn_out": nrm(ks[19], (L, D_FF, D_MODEL), D_FF ** -0.5),
        "norm_ffn_post": gain(ks[20], (L, D_MODEL)),
    }


def reference(x, norm_mix_pre, w_in, gmlp_norm_g, gmlp_norm_b, gmlp_w_s, gmlp_b_s,
              lambda_q1, lambda_k1, lambda_q2, lambda_k2, diff_subln_g, pool_w,
              pool_scale, w_branch, w_out, norm_mix_post, norm_ffn_pre, w_ffn_in,
              w_ffn_out, norm_ffn_post):
    bsz, s, _ = x.shape
    for l in range(DEPTH):
        h = rmsnorm(x, norm_mix_pre[l])
        z = h @ w_in[l]
        za, zq, zk, zv, zc, zg = jnp.split(z, SPLITS, axis=-1)
        ya = gmlp_branch(jax.nn.gelu(za, approximate=False), gmlp_norm_g[l],
                         gmlp_norm_b[l], gmlp_w_s[l], gmlp_b_s[l])
        yb = diff_attention(zq, zk, zv, lambda_q1[l], lambda_k1[l], lambda_q2[l],
                            lambda_k2[l], diff_subln_g[l], lambda_init_fn(l))
        yc = pool_branch(zc, pool_w[l], pool_scale[l])
        ys = jnp.stack([ya, yb, yc], axis=2)
        up = jnp.einsum('bsnw,nwd->bsnd', ys, w_branch[l])
        gates = jax.nn.sigmoid(zg).reshape(bsz, s, N_BRANCH, D_MODEL)
        merged = jnp.sum(gates * up, axis=2)
        x = x + rmsnorm(merged @ w_out[l], norm_mix_post[l])
        h = rmsnorm(x, norm_ffn_pre[l])
        g, u = jnp.split(h @ w_ffn_in[l], 2, axis=-1)
        f = (jax.nn.silu(g) * u) @ w_ffn_out[l]
        x = x + rmsnorm(f, norm_ffn_post[l])
    return x
```

```python
import math
from contextlib import ExitStack

import numpy as np
import concourse.bass as bass
import concourse.mybir as mybir
from concourse.bass_utils import run_bass_kernel_spmd

F32 = mybir.dt.float32
BF16 = mybir.dt.bfloat16
AF = mybir.ActivationFunctionType
ALU = mybir.AluOpType

D = 1024
S_LEN = 2048
DFF = 2816
INT = 6144
EPS = 1e-6
NSLOT = 3
SLOT = 4096
NCOL = 40
IMMEDIATE_RESID = False
C_WINDOWS = (2, 4, 8, 16)


def lambda_init_fn(layer_idx):
    return 0.8 - 0.6 * math.exp(-0.3 * layer_idx)


class Buf:
    __slots__ = ("name", "w", "rs")

    def __init__(self, name):
        self.name = name
        self.w = None
        self.rs = {}


class Sched:
    ENG = ("pe", "act", "dve", "pool", "sp")

    def __init__(self):
        self.q = {e: [] for e in self.ENG}
        self.cnt = {e: 0 for e in self.ENG}
        self.seen = {e: {} for e in self.ENG}
        self.dmacnt = {}

    def _waits(self, eng, reads, writes):
        need = {}

        def add(k, v):
            if k == eng and eng == "pe":
                return
            if need.get(k, 0) < v:
                need[k] = v

        for b in reads:
            if b.w is not None:
                add(*b.w)
        for b in writes:
            if b.w is not None:
                add(*b.w)
            for k, v in b.rs.items():
                add(k, v)
        out = []
        for k, v in need.items():
            if self.seen[eng].get(k, 0) < v:
                self.seen[eng][k] = v
                out.append((k, v))
        return out

    def _commit(self, tok, reads, writes):
        k, v = tok
        for b in reads:
            if b.rs.get(k, 0) < v:
                b.rs[k] = v
        for b in writes:
            b.w = tok
            b.rs = {}

    def op(self, eng, fn, reads=(), writes=()):
        waits = self._waits(eng, reads, writes)
        self.cnt[eng] += 1
        tok = (eng, self.cnt[eng])
        self.q[eng].append((waits, fn, eng, 1))
        self._commit(tok, reads, writes)
        return tok

    def dma(self, eng, semkey, fn, reads=(), writes=()):
        waits = self._waits(eng, reads, writes)
        prev = self.dmacnt.get(semkey, 0)
        if prev > self.seen[eng].get(semkey, 0):
            self.seen[eng][semkey] = prev
            waits.append((semkey, prev))
        self.dmacnt[semkey] = prev + 16
        tok = (semkey, prev + 16)
        self.q[eng].append((waits, fn, semkey, 16))
        self._commit(tok, reads, writes)
        return tok

    def barrier(self, engs=("act", "dve")):
        snap = {e: self.cnt[e] for e in engs}
        for e in engs:
            waits = []
            for k, v in snap.items():
                if k != e and v > self.seen[e].get(k, 0):
                    self.seen[e][k] = v
                    waits.append((k, v))
            self.q[e].append((waits, None, None, 0))


def build_program(layers, first, last):
    nc = bass.Bass("TRN2", target_bir_lowering=False)
    dt_in = lambda name, shape: nc.dram_tensor(name, shape, F32, kind="ExternalInput").ap()
    xTd = dt_in("xT", [D, S_LEN])
    w_in = dt_in("w_in", [2, D, INT])
    w_branch = dt_in("w_branch", [2, 3, 512, D])
    w_out = dt_in("w_out", [2, D, D])
    w_ffn_in = dt_in("w_ffn_in", [2, D, 2 * DFF])
    w_ffn_out = dt_in("w_ffn_out", [2, DFF, D])
    colsd = dt_in("cols", [128, 2 * NCOL])
    bcd = dt_in("bc", [2, 1024])
    lamd = dt_in("lam", [2, 256])
    wsTd = dt_in("wsT", [2, 128, 512])
    bsd = dt_in("bs", [2, 512])
    wpd = dt_in("wp", [2, 128, 512])
    trid = dt_in("tri", [128, 128])
    pmd = dt_in("pm", [128, 12 * 128])
    outd = nc.dram_tensor("outT", [D, S_LEN], F32, kind="ExternalOutput").ap()

    S = Sched()
    es = ExitStack()
    sb = lambda name, shape, dt: es.enter_context(nc.sbuf_tensor(name, shape, dt))

    xT = sb("xT_s", [128, 8, S_LEN], F32)
    hT = sb("hT", [128, 8, 1024], BF16)
    U = sb("U", [128, 40960], BF16)
    WS = sb("WS", [128, NSLOT, SLOT], BF16)
    T = sb("T", [128, 8, 512], BF16)
    carry = sb("carry", [128, 512], BF16)
    cols = sb("cols_s", [128, 2 * NCOL], F32)
    gbc = sb("gbc", [128, 2, 512], F32)
    lamb = sb("lamb", [128, 256], F32)
    lamc = sb("lamc", [128, 8], F32)
    stat = sb("stat", [128, 16], F32)
    stat2 = sb("stat2", [128, 8, 10], F32)
    wsT = sb("wsT_s", [128, 4, 128], BF16)
    wpl = sb("wpl", [128, 4, 128], BF16)
    rows = sb("rows", [1, 640], BF16)
    ones = sb("ones", [128, 128], BF16)
    tri = sb("tri_s", [128, 128], BF16)
    pm = sb("pm_s", [128, 12, 128], BF16)
    ps = es.enter_context(nc.psum_tensor("ps", [128, 8, 512], F32))

    sem_names = (["pe", "act", "dve", "pool", "XIN0", "XIN1", "XIN2", "XIN3", "OUT0", "OUT1"] + [f"PS{i}" for i in range(3)]
                 + [f"PP{i}" for i in range(5)] + [f"WS{i}_{j}" for i in range(NSLOT) for j in range(3)])
    sems = {n: es.enter_context(nc.semaphore(n)) for n in sem_names}
    es.enter_context(nc.allow_low_precision("bf16 matmul operands, fp32 accumulation"))

    def uview(a, b):
        return U[:, a:b]

    KT = uview(0, 8192).rearrange("p (h t) -> p h t", h=4)
    V = uview(8192, 16384).rearrange("p (c e) -> p c e", c=16)
    yT = uview(16384, 28672).rearrange("p (c t) -> p c t", c=12)
    RQ = uview(28672, 32768).rearrange("p (c t) -> p c t", c=4)
    RMm = uview(32768, 40960).rearrange("p (c t) -> p c t", c=8)
    RMp = uview(32768, 36864).rearrange("p (c e) -> p c e", c=8)
    aT = uview(0, 22528).rearrange("p (c t) -> p c t", c=22)
    fT = uview(22528, 38912).bitcast(F32).rearrange("p (c t) -> p c t", c=8)
    oT = uview(16384, 24576).bitcast(F32).rearrange("p (c t) -> p c t", c=8)

    def Tb16(i):
        return T[:, i, :]

    def Tf32(i):
        return T[:, i:i + 2, :].rearrange("p a b -> p (a b)").bitcast(F32)

    xb = [[Buf(f"x{c}_{t}") for t in range(4)] for c in range(8)]
    hb = [Buf("h0"), Buf("h1")]
    KTb = [Buf(f"K{t}") for t in range(4)]
    Vb = [Buf(f"V{t}") for t in range(4)]
    yb = [[Buf(f"y{i}_{t}") for t in range(2)] for i in range(3)]
    RQb = Buf("RQ")
    RMb = Buf("RM")
    oTb = [Buf(f"oT{i}") for i in range(8)]
    aTb = [Buf("a0"), Buf("a1")]
    fTb = [[Buf(f"f{t}_{i}") for i in range(8)] for t in range(2)]
    Tbuf = [Buf(f"T{i}") for i in range(8)]
    carryb = Buf("carry")
    constb = Buf("const")
    lpb = Buf("lp")
    statb = Buf("stat")
    bankb = [Buf(f"bank{i}") for i in range(8)]
    slotb = [[Buf(f"slot{i}_{j}") for j in range(3)] for i in range(NSLOT)]
    outb = Buf("out")

    pinned = set()
    bank_ptr = [0]

    def alloc_bank(pin=False):
        for _ in range(16):
            b = bank_ptr[0]
            bank_ptr[0] = (b + 1) % 8
            if b not in pinned:
                if pin:
                    pinned.add(b)
                return b
        raise RuntimeError("no psum bank")

    def unpin(b):
        pinned.discard(b)

    pair_ptr = [0]

    def alloc_pair(pin=False):
        for _ in range(8):
            k = pair_ptr[0]
            pair_ptr[0] = (k + 1) % 4
            if 2 * k not in pinned and 2 * k + 1 not in pinned:
                if pin:
                    pinned.add(2 * k)
                    pinned.add(2 * k + 1)
                return 2 * k
        raise RuntimeError("no psum bank pair")

    loads = []
    wstate = {"issued": 0, "released": 0, "acq": 0}

    def issue_loads():
        while wstate["issued"] < len(loads) and wstate["issued"] < wstate["released"] + NSLOT:
            m = wstate["issued"]
            s = m % NSLOT
            for i, (dst, src_) in enumerate(loads[m]):
                o = dst(WS[:, s, :])
                S.dma("pool", f"WS{s}_{i}", (lambda e, o=o, src_=src_: e.dma_start(out=o, in_=src_)), reads=(), writes=[slotb[s][i]])
            wstate["issued"] += 1

    def acquire():
        k = wstate["acq"]
        wstate["acq"] += 1
        issue_loads()
        assert wstate["issued"] > k, (k, wstate)
        s = k % NSLOT
        return WS[:, s, :], slotb[s]

    def release():
        wstate["released"] += 1
        issue_loads()

    def v3(kc):
        return lambda sl: sl.rearrange("p (k c) -> p k c", k=kc)

    def plan_loads(l):
        wi = w_in[l].rearrange("(k p) c -> p k c", p=128)
        wfi = w_ffn_in[l].rearrange("(k p) c -> p k c", p=128)
        wfo = w_ffn_out[l].rearrange("(k p) c -> p k c", p=128)
        wo = w_out[l].rearrange("(k p) c -> p k c", p=128)
        mix = []
        for c0 in (1536, 2048, 1024, 0, 512, 2560):
            mix.append([(v3(8), wi[:, :, c0:c0 + 512])])
        for dc in range(8):
            g = []
            for i in range(3):
                c0 = 3072 + i * 1024 + dc * 128
                g.append((lambda sl, i=i: sl[:, 0:3072].rearrange("p (i k c) -> p i k c", i=3, k=8)[:, i],
                          wi[:, :, c0:c0 + 128]))
            mix.append(g)
            b = []
            for i in range(3):
                wb = w_branch[l, i].rearrange("(k p) c -> p k c", p=128)
                b.append((lambda sl, i=i: sl[:, 0:1536].rearrange("p (i k c) -> p i k c", i=3, k=4)[:, i],
                          wb[:, :, dc * 128:(dc + 1) * 128]))
            mix.append(b)
        mix.append([(v3(8), wo[:, :, 0:512])])
        mix.append([(v3(8), wo[:, :, 512:1024])])
        ffn = []
        for jp in range(11):
            ffn.append([
                (lambda sl: sl.rearrange("p (k a c) -> p k a c", k=8, a=2)[:, :, 0, :], wfi[:, :, jp * 256:(jp + 1) * 256]),
                (lambda sl: sl.rearrange("p (k a c) -> p k a c", k=8, a=2)[:, :, 1, :],
                 wfi[:, :, DFF + jp * 256:DFF + (jp + 1) * 256]),
            ])
        for oc in range(8):
            ffn.append([(lambda sl: sl[:, 0:2816].rearrange("p (k c) -> p k c", k=22), wfo[:, :, oc * 128:(oc + 1) * 128])])
        return mix + mix + ffn + ffn

    for l in layers:
        loads.extend(plan_loads(l))

    def mm_groups(groups, reads, writes):
        def fn(e, groups=groups):
            ins = None
            for out_ap, pairs in groups:
                n = len(pairs)
                for i, (lh, rh) in enumerate(pairs):
                    ins = e.matmul(out_ap, lh, rh, start=(i == 0), stop=(i == n - 1))
            return ins

        return S.op("pe", fn, reads=reads, writes=writes)

    def mm_raw(items, reads, writes):
        def fn(e, items=items):
            ins = None
            for o, lh, rh, st, sp in items:
                ins = e.matmul(o, lh, rh, start=st, stop=sp)
            return ins

        return S.op("pe", fn, reads=reads, writes=writes)

    def act(out, in_, func, reads, writes, scale=1.0, bias=0.0):
        return S.op("act", lambda e: e.activation(out=out, in_=in_, func=func, scale=scale, bias=bias),
                    reads=reads, writes=writes)

    class _Rec:
        def __getattr__(self, name):
            def f(*a, **kw):
                self.call = (name, a, kw)
            return f

    def dve(fnc, reads, writes):
        r = _Rec()
        fnc(r)
        name, a, kw = r.call
        return S.op("dve", lambda e: getattr(e, name)(*a, **kw), reads=reads, writes=writes)

    def dv(method, reads, writes, **kw):
        return S.op("dve", lambda e: getattr(e, method)(**kw), reads=reads, writes=writes)

    def evac_copy(idx, out, in_, reads, writes):
        if idx % 2 == 0:
            act(out, in_, AF.Copy, reads, writes)
        else:
            dve(lambda e: e.tensor_copy(out=out, in_=in_), reads, writes)

    def bank_ap(b):
        return ps[:, b, :]

    def rstd_ops(out, bank, n, inv_n, post_mul, reads, writes):
        src_ = ps[:, bank, 0:n]
        act(out, src_, AF.Ln, reads, writes, scale=inv_n, bias=EPS)
        act(out, out, AF.Exp, writes, writes, scale=-0.5, bias=(0.0 if post_mul is None else math.log(post_mul)))

    S.dma("sp", "PS0", lambda e: e.dma_start(out=cols[:], in_=colsd), writes=[constb])
    S.dma("pool", "PP0", lambda e: e.dma_start(out=tri[:], in_=trid), writes=[constb])
    S.dma("pool", "PP1", lambda e: e.dma_start(out=pm[:].rearrange("p a b -> p (a b)"), in_=pmd), writes=[constb])
    dve(lambda e: e.memset(ones[:], 1.0), [], [constb])
    dve(lambda e: e.memset(rows[0:1, 512:640], 1.0), [], [constb])
    late_x = []
    if first:
        for t4 in range(4):
            def emit_x(t4=t4):
                src_ = xTd.rearrange("(c p) t -> p c t", p=128)[:, :, t4 * 512:(t4 + 1) * 512]
                S.dma("sp", f"XIN{t4}", (lambda e: e.dma_start(out=xT[:, :, t4 * 512:(t4 + 1) * 512], in_=src_)),
                      reads=([KTb[0]] if t4 >= 2 else []), writes=[xb[c][t4] for c in range(8)])
            if t4 < 2:
                emit_x()
            else:
                late_x.append(emit_x)

    def layer_setup(l):
        rd = [lpb]
        S.dma("sp", "PS1", lambda e: e.dma_start(out=gbc[:].rearrange("p a b -> p (a b)"), in_=bcd[l:l + 1, :].partition_broadcast(128)), writes=rd)
        S.dma("sp", "PS2", lambda e: e.dma_start(out=lamb[:], in_=lamd[l:l + 1, :].partition_broadcast(128)), writes=rd)
        S.dma("pool", "PP2", lambda e: e.dma_start(out=wsT[:].rearrange("p a b -> p (a b)"), in_=wsTd[l]), writes=rd)
        S.dma("pool", "PP3", lambda e: e.dma_start(out=wpl[:].rearrange("p a b -> p (a b)"), in_=wpd[l]), writes=rd)
        S.dma("pool", "PP4", lambda e: e.dma_start(out=rows[0:1, 0:512], in_=bsd[l:l + 1, :]), writes=rd)
        for g in range(4):
            dve(lambda e, g=g: e.tensor_tensor(out=wsT[:, g, :], in0=wsT[:, g, :], in1=tri[:], op=ALU.mult), [lpb, constb], [lpb])
        li = lambda_init_fn(l)
        tmp = Tf32(0)
        dve(lambda e: e.tensor_tensor(out=tmp[:, 0:64], in0=lamb[:, 0:64], in1=lamb[:, 64:128], op=ALU.mult), [lpb], [Tbuf[0], Tbuf[1]])
        dve(lambda e: e.tensor_tensor(out=tmp[:, 64:128], in0=lamb[:, 128:192], in1=lamb[:, 192:256], op=ALU.mult), [lpb], [Tbuf[0], Tbuf[1]])
        dve(lambda e: e.reduce_sum(out=lamc[:, 2:3], in_=tmp[:, 0:64], axis=mybir.AxisListType.X), [Tbuf[0]], [statb])
        dve(lambda e: e.reduce_sum(out=lamc[:, 3:4], in_=tmp[:, 64:128], axis=mybir.AxisListType.X), [Tbuf[0]], [statb])
        act(lamc[:, 4:6], lamc[:, 2:4], AF.Exp, [statb], [statb])
        dve(lambda e: e.tensor_tensor(out=lamc[:, 6:7], in0=lamc[:, 5:6], in1=lamc[:, 4:5], op=ALU.subtract), [statb], [statb])
        dve(lambda e: e.tensor_scalar(out=lamc[:, 0:1], in0=lamc[:, 6:7], scalar1=-li, scalar2=None, op0=ALU.add), [statb], [lpb])

    def phase_norm(l, half, colbase):
        for tile in range(2):
            tg = half * 1024 + tile * 512
            t4 = half * 2 + tile
            b = alloc_bank()
            sq = T[:, 0:4, :]
            for r in range(2):
                act(sq, xT[:, 4 * r:4 * r + 4, tg:tg + 512], AF.Square,
                    [xb[c][t4] for c in range(4 * r, 4 * r + 4)], Tbuf[0:4])
                mm_raw([(bank_ap(b), ones[:], sq[:, k, :], (r == 0 and k == 0), (r == 1 and k == 3)) for k in range(4)],
                       Tbuf[0:4] + [constb], [bankb[b]])
            blk = 4 + 2 * tile
            rs = Tf32(blk)
            rstd_ops(rs, b, 512, 1.0 / D, None, [bankb[b]], [Tbuf[blk], Tbuf[blk + 1]])
            for c in range(8):
                dve(lambda e, c=c: e.scalar_tensor_tensor(out=hT[:, c, tile * 512:(tile + 1) * 512], in0=xT[:, c, tg:tg + 512],
                                                           scalar=cols[:, colbase + c:colbase + c + 1], in1=rs,
                                                           op0=ALU.mult, op1=ALU.mult),
                    [xb[c][t4], Tbuf[blk], Tbuf[blk + 1], constb], [hb[tile]])

    def proj_feat(slot, slb, ncol, dest_fn, func, scale, writes_fn):
        sv = slot.rearrange("p (k c) -> p k c", k=8)
        n = 0
        for g in range(ncol):
            for tile in range(2):
                b = alloc_bank()
                mm_groups([(bank_ap(b), [(sv[:, kc, g * 128:(g + 1) * 128], hT[:, kc, tile * 512:(tile + 1) * 512]) for kc in range(8)])],
                          slb + [hb[tile]], [bankb[b]])
                out = dest_fn(g, tile)
                if func is None:
                    evac_copy(n, out, bank_ap(b), [bankb[b]], writes_fn(g, tile))
                else:
                    act(out, bank_ap(b), func, [bankb[b]], writes_fn(g, tile), scale=scale)
                n += 1
                drain(1)

    def proj_tok(slot, slb, chunk):
        sv = slot.rearrange("p (k c) -> p k c", k=8)
        b = alloc_bank()
        tile = chunk // 4
        mm_groups([(bank_ap(b), [(hT[:, kc, chunk * 128:(chunk + 1) * 128], sv[:, kc, :]) for kc in range(8)])],
                  slb + [hb[tile]], [bankb[b]])
        return b

    def phase_attn(l, half):
        li = lambda_init_fn(l)
        cb = l * NCOL
        slot, slb = acquire()
        proj_feat(slot, slb, 4, lambda h, tile: KT[:, h, half * 1024 + tile * 512: half * 1024 + (tile + 1) * 512], None, 1.0,
                  lambda h, tile: [KTb[half * 2 + tile]])
        release()
        while late_x:
            late_x.pop(0)()
        slot, slb = acquire()
        for chunk in range(8):
            b = proj_tok(slot, slb, chunk)
            evac_copy(chunk, V[:, half * 8 + chunk, :], bank_ap(b), [bankb[b]], [Vb[half * 2 + chunk // 4]])
            drain(1)
        release()
        drain(10 ** 6)
        slot, slb = acquire()
        proj_feat(slot, slb, 4, lambda h, tile: RQ[:, h, tile * 512:(tile + 1) * 512], AF.Copy, 0.125, lambda h, tile: [RQb])
        release()
        iters = []
        for h in range(4):
            for qt in range(4):
                gq0 = half * 8 + qt * 2
                njs = gq0 + 2
                for j in range(njs):
                    iters.append((h, qt, j, njs))
        n_it = len(iters)

        def emit_qk(i):
            h, qt, j, njs = iters[i]
            q0 = qt * 256
            gq0 = half * 8 + qt * 2
            c0 = 0 if j <= gq0 else 128
            sbk = 2 * (i % 2)
            items = []
            for c in range(2):
                items.append((ps[:, sbk + c, c0:256], KT[c * 64:(c + 1) * 64, h, j * 128:(j + 1) * 128],
                              RQ[c * 64:(c + 1) * 64, h, q0 + c0:q0 + 256], True, True))
            mm_raw(items, [KTb[j // 4], RQb], [bankb[sbk], bankb[sbk + 1]])

        pending = []
        emit_qk(0)
        pendA = []
        for i in range(n_it):
            h, qt, j, njs = iters[i]
            q0 = qt * 256
            gq0 = half * 8 + qt * 2
            c0 = 0 if j <= gq0 else 128
            sbk = 2 * (i % 2)
            Ob = 4 + 2 * ((h * 4 + qt) % 2)
            Lb = Ob + 1
            O3 = ps[:, Ob, :].rearrange("p (a q) -> p a q", a=2)
            L3 = ps[:, Lb, :].rearrange("p (a q) -> p a q", a=2)
            if i + 1 < n_it:
                emit_qk(i + 1)
            S3 = ps[:, sbk:sbk + 2, 0:256]
            pblk = i % 3
            P3 = T[:, pblk, :].rearrange("p (a q) -> p a q", a=2)
            act(P3[:, :, c0:256], S3[:, :, c0:256], AF.Exp, [bankb[sbk], bankb[sbk + 1]], [Tbuf[pblk]])
            if j >= gq0:
                for c in range(2):
                    dve(lambda e: e.tensor_tensor(out=P3[:, c, c0:c0 + 128], in0=P3[:, c, c0:c0 + 128], in1=tri[:], op=ALU.mult),
                        [Tbuf[pblk], constb], [Tbuf[pblk]])
            vj = V[:, j, h * 128:(h + 1) * 128]
            if c0 == 0:
                pv = [(O3[:, :, :], vj, P3[:, :, :], j == 0, j == njs - 1),
                      (L3[:, :, :], ones[:], P3[:, :, :], j == 0, j == njs - 1)]
            else:
                pv = []
                for c in range(2):
                    pv.append((O3[:, c, c0:256], vj, P3[:, c, c0:256], False, c == 1 and j == njs - 1))
                    pv.append((L3[:, c, c0:256], ones[:], P3[:, c, c0:256], False, c == 1 and j == njs - 1))
            mm_raw(pv, [Vb[j // 4], Tbuf[pblk], constb], [bankb[Ob], bankb[Lb]])
            if pending and i - pending[0][0] >= 8:
                pending.pop(0)[1]()
            if pendA and pendA[0][0] < i:
                pendA.pop(0)[1]()
            if j != njs - 1:
                continue

            def stage_a(i=i, h=h, qt=qt, q0=q0, Ob=Ob, Lb=Lb):
              if pending:
                pending.pop(0)[1]()
              Ocp = Tf32(3)
              Lcp = Tf32(5)
              act(Lcp, bank_ap(Lb), AF.Ln, [bankb[Lb]], [Tbuf[5], Tbuf[6]])
              act(Lcp, Lcp, AF.Exp, [Tbuf[5], Tbuf[6]], [Tbuf[5], Tbuf[6]], scale=-1.0)
              dve(lambda e: e.tensor_tensor(out=Ocp, in0=bank_ap(Ob), in1=Lcp, op=ALU.mult),
                  [bankb[Ob], Tbuf[5], Tbuf[6]], [Tbuf[3], Tbuf[4]])
              diff = Lcp[:, 0:256]
              dve(lambda e: e.scalar_tensor_tensor(out=diff, in0=Ocp[:, 256:512], scalar=lamc[:, 0:1], in1=Ocp[:, 0:256],
                                                   op0=ALU.mult, op1=ALU.add),
                  [Tbuf[3], Tbuf[4], lpb], [Tbuf[5]])
              sqd = T[:, 6, 0:256]
              dve(lambda e: e.tensor_tensor(out=sqd, in0=diff, in1=diff, op=ALU.mult), [Tbuf[5]], [Tbuf[6]])

              def stage_b(h=h, qt=qt, q0=q0, diff=diff, sqd=sqd, Ocp=Ocp):
                  ssb = 7 if False else 1
                  mm_raw([(ps[:, ssb, 256:512], ones[:], sqd, True, True)], [Tbuf[6], constb], [bankb[ssb]])
                  rstd = Ocp[:, 256:512]
                  act(rstd, ps[:, ssb, 256:512], AF.Ln, [bankb[ssb]], [Tbuf[4]], scale=1.0 / 128, bias=EPS)
                  act(rstd, rstd, AF.Exp, [Tbuf[4]], [Tbuf[4]], scale=-0.5, bias=math.log(1.0 - li))
                  dve(lambda e: e.scalar_tensor_tensor(out=yT[:, 4 + h, q0:q0 + 256], in0=diff, scalar=cols[:, cb + 36:cb + 37], in1=rstd,
                                                       op0=ALU.mult, op1=ALU.mult),
                      [Tbuf[5], Tbuf[4], constb], [yb[1][qt // 2]])

              pending.append((i, stage_b))
            pendA.append((i, stage_a))
        while pendA:
            pendA.pop(0)[1]()
        while pending:
            pending.pop(0)[1]()

    def phase_gmlp(l, half):
        slot, slb = acquire()
        proj_feat(slot, slb, 4, lambda g, tile: RQ[:, g, tile * 512:(tile + 1) * 512], AF.Gelu, 1.0, lambda g, tile: [RQb])
        release()
        slot, slb = acquire()
        vall = RMm.rearrange("p c t -> p (c t)").bitcast(F32).rearrange("p (c e) -> p c e", c=8)
        for chunk in range(8):
            b = proj_tok(slot, slb, chunk)
            act(vall[:, chunk, :], bank_ap(b), AF.Gelu, [bankb[b]], [RMb])
            dve(lambda e: e.bn_stats(out=stat2[:, chunk, 0:6], in_=vall[:, chunk, :]), [RMb], [statb])
            dve(lambda e: e.bn_aggr(out=stat2[:, chunk, 6:8], in_=stat2[:, chunk, 0:6]), [statb], [statb])
        release()
        act(stat2[:, :, 8:9], stat2[:, :, 7:8], AF.Sqrt, [statb], [statb], scale=1.0, bias=EPS)
        dve(lambda e: e.reciprocal(out=stat2[:, :, 8:9], in_=stat2[:, :, 8:9]), [statb], [statb])
        dve(lambda e: e.tensor_tensor(out=stat2[:, :, 9:10], in0=stat2[:, :, 6:7], in1=stat2[:, :, 8:9], op=ALU.mult), [statb], [statb])
        dve(lambda e: e.tensor_scalar(out=stat2[:, :, 9:10], in0=stat2[:, :, 9:10], scalar1=-1.0, scalar2=None, op0=ALU.mult),
            [statb], [statb])
        for chunk in range(8):
            vf = vall[:, chunk, :]
            dve(lambda e: e.tensor_scalar(out=vf, in0=vf, scalar1=stat2[:, chunk, 8:9], scalar2=stat2[:, chunk, 9:10],
                                          op0=ALU.mult, op1=ALU.add), [RMb, statb], [RMb])
            dve(lambda e: e.tensor_tensor(out=vf, in0=vf, in1=gbc[:, 0, :], op=ALU.mult), [RMb, lpb], [RMb])
            nblk = 4 + (chunk % 2)
            vn = T[:, nblk, :]
            dve(lambda e: e.tensor_tensor(out=vn, in0=vf, in1=gbc[:, 1, :], op=ALU.add), [RMb, lpb], [Tbuf[nblk]])
            mb = alloc_bank()
            items = []
            for g in range(4):
                o = ps[:, mb, g * 128:(g + 1) * 128]
                items.append((o, vn[:, g * 128:(g + 1) * 128], wsT[:, g, :], True, False))
                items.append((o, rows[0:1, 512:640], rows[0:1, g * 128:(g + 1) * 128], False, True))
            mm_raw(items, [Tbuf[nblk], lpb, constb], [bankb[mb]])
            tile = chunk // 4
            dve(lambda e: e.tensor_tensor(
                out=yT[:, 0:4, chunk * 128:(chunk + 1) * 128], in0=ps[:, mb, :].rearrange("p (g t) -> p g t", g=4),
                in1=RQ[:, :, chunk * 128:(chunk + 1) * 128], op=ALU.mult),
                [bankb[mb], RQb], [yb[0][tile]])

    def phase_pool(l, half):
        cb = l * NCOL
        slot, slb = acquire()
        for chunk in range(8):
            b = proj_tok(slot, slb, chunk)
            evac_copy(chunk, RMp[:, chunk, :], bank_ap(b), [bankb[b]], [RMb])
        release()
        for chunk in range(8):
            gchunk = half * 8 + chunk
            b = alloc_bank()
            items = []
            for g in range(4):
                o = ps[:, b, g * 128:(g + 1) * 128]
                cur = RMp[:, chunk, g * 128:(g + 1) * 128]
                if gchunk == 0:
                    items.append((o, cur, pm[:, 8 + g, :], True, True))
                else:
                    prev = carry[:, g * 128:(g + 1) * 128] if chunk == 0 else RMp[:, chunk - 1, g * 128:(g + 1) * 128]
                    items.append((o, cur, pm[:, g, :], True, False))
                    items.append((o, prev, pm[:, 4 + g, :], False, True))
            mm_raw(items, [RMb, carryb, constb], [bankb[b]])
            evac_copy(chunk, RQ[:, :, chunk * 128:(chunk + 1) * 128], ps[:, b, :].rearrange("p (g t) -> p g t", g=4),
                      [bankb[b]], [RQb])
        if half == 0:
            dve(lambda e: e.tensor_copy(out=carry[:], in_=RMp[:, 7, :]), [RMb], [carryb])
        for g in range(4):
            for tile in range(2):
                b = alloc_bank()
                mm_raw([(bank_ap(b), wpl[:, g, :], RQ[:, g, tile * 512:(tile + 1) * 512], True, True)], [lpb, RQb], [bankb[b]])
                act(yT[:, 8 + g, tile * 512:(tile + 1) * 512], bank_ap(b), AF.Copy, [bankb[b], constb], [yb[2][tile]],
                    scale=cols[:, cb + 32 + g:cb + 33 + g])

    def phase_merge(l, half):
        for dc in range(8):
            gsl, gsb = acquire()
            bsl, bsb = acquire()
            gv = gsl[:, 0:3072].rearrange("p (i k c) -> p i k c", i=3, k=8)
            bv = bsl[:, 0:1536].rearrange("p (i k c) -> p i k c", i=3, k=4)
            for tile in range(2):
                tsl = slice(tile * 512, (tile + 1) * 512)
                gb = [Tf32(0), Tf32(2)]
                gbb = [[Tbuf[0], Tbuf[1]], [Tbuf[2], Tbuf[3]]]
                tmp = Tf32(4)
                tmpb = [Tbuf[4], Tbuf[5]]
                m = Tf32(6)
                mbuf = [Tbuf[6], Tbuf[7]]
                Gbs = []
                for i in range(3):
                    Gb = alloc_bank()
                    mm_groups([(bank_ap(Gb), [(gv[:, i, kc, :], hT[:, kc, tsl]) for kc in range(8)])], gsb + [hb[tile]], [bankb[Gb]])
                    Gbs.append(Gb)
                for i in range(3):
                    Gb = Gbs[i]
                    Ub = alloc_bank()
                    mm_groups([(bank_ap(Ub), [(bv[:, i, wc, :], yT[:, i * 4 + wc, tsl]) for wc in range(4)])],
                              bsb + [yb[i][tile]], [bankb[Ub]])
                    k = i % 2
                    act(gb[k], bank_ap(Gb), AF.Sigmoid, [bankb[Gb]], gbb[k])
                    if i == 0:
                        dve(lambda e: e.tensor_tensor(out=m, in0=bank_ap(Ub), in1=gb[k], op=ALU.mult),
                            [bankb[Ub]] + gbb[k], mbuf)
                    else:
                        dve(lambda e: e.tensor_tensor(out=tmp, in0=bank_ap(Ub), in1=gb[k], op=ALU.mult),
                            [bankb[Ub]] + gbb[k], tmpb)
                        if i == 1:
                            dve(lambda e: e.tensor_tensor(out=m, in0=m, in1=tmp, op=ALU.add), mbuf + tmpb, mbuf)
                        else:
                            dve(lambda e: e.tensor_tensor(out=RMm[:, dc, tsl], in0=m, in1=tmp, op=ALU.add),
                                mbuf + tmpb, [RMb])
            release()
            release()

    bg = []

    def drain(n=1):
        for _ in range(n):
            if bg:
                bg.pop(0)()

    def resid_update(l, half, tile, src_fn, srcbufs, ssb, colbase, final, tmpB):
        tg = half * 1024 + tile * 512
        t4 = half * 2 + tile
        blk = 4 + 2 * tile
        rs = Tf32(blk)
        rstd_ops(rs, ssb, 512, 1.0 / D, None, [bankb[ssb]], [Tbuf[blk], Tbuf[blk + 1]])

        def unit(oc):
            sa = src_fn(oc)
            dve(lambda e: e.tensor_tensor(out=sa, in0=sa, in1=rs, op=ALU.mult),
                srcbufs(oc) + [Tbuf[blk], Tbuf[blk + 1]], srcbufs(oc))
            dve(lambda e: e.scalar_tensor_tensor(out=xT[:, oc, tg:tg + 512], in0=sa, scalar=cols[:, colbase + oc:colbase + oc + 1],
                                                 in1=xT[:, oc, tg:tg + 512], op0=ALU.mult, op1=ALU.add),
                srcbufs(oc) + [constb], [xb[oc][t4]])

        for oc in range(8):
            bg.append(lambda oc=oc: unit(oc))
        if final:
            def out_unit():
                dst = outd.rearrange("(c p) t -> p c t", p=128)[:, :, tg:tg + 512]
                S.dma("sp", f"OUT{tile}", lambda e: e.dma_start(out=dst, in_=xT[:, :, tg:tg + 512]),
                      reads=[xb[c][t4] for c in range(8)], writes=[outb])
            bg.append(out_unit)
        if IMMEDIATE_RESID:
            drain(10 ** 6)

    def phase_wout(l, half):
        cb = l * NCOL
        sA, sAb = acquire()
        sB, sBb = acquire()
        vA = sA.rearrange("p (k c) -> p k c", k=8)
        vB = sB.rearrange("p (k c) -> p k c", k=8)
        ybufs = [yb[0][0], yb[0][1], yb[1][0], yb[1][1]]
        for tile in range(2):
            tsl = slice(tile * 512, (tile + 1) * 512)
            ssb = alloc_bank(pin=True)
            for oc in range(8):
                wv = vA if oc < 4 else vB
                wb = sAb if oc < 4 else sBb
                b = alloc_bank()
                mm_groups([(bank_ap(b), [(wv[:, kc, (oc % 4) * 128:(oc % 4 + 1) * 128], RMm[:, kc, tsl]) for kc in range(8)])],
                          wb + [RMb], [bankb[b]])
                if oc > 0:
                    pb = (oc - 1) % 2
                    mm_raw([(bank_ap(ssb), ones[:], T[:, pb, :], oc - 1 == 0, False)], [Tbuf[pb], constb], [bankb[ssb]])
                act(oT[:, oc, :], bank_ap(b), AF.Copy, [bankb[b]], [oTb[oc]] + ybufs)
                sblk = oc % 2
                act(T[:, sblk, :], bank_ap(b), AF.Square, [bankb[b]], [Tbuf[sblk]])
            mm_raw([(bank_ap(ssb), ones[:], T[:, 7 % 2, :], False, True)], [Tbuf[7 % 2], constb], [bankb[ssb]])
            resid_update(l, half, tile, lambda oc: oT[:, oc, :], lambda oc: [oTb[oc]], ssb, cb + 8, False, 6 if tile == 0 else 4)
            if tile == 0:
                drain(10 ** 6)
            unpin(ssb)
        release()
        release()

    def phase_ffn_in(l, half):
        it = 0
        for jp in range(11):
            if jp == 4:
                drain(10 ** 6)
            slot, slb = acquire()
            sv = slot.rearrange("p (k a c) -> p k a c", k=8, a=2)
            for jj in range(2):
                j = 2 * jp + jj
                for tile in range(2):
                    tsl = slice(tile * 512, (tile + 1) * 512)
                    Gb = alloc_bank()
                    mm_groups([(bank_ap(Gb), [(sv[:, kc, 0, jj * 128:(jj + 1) * 128], hT[:, kc, tsl]) for kc in range(8)])],
                              slb + [hb[tile]], [bankb[Gb]])
                    Ub = alloc_bank()
                    mm_groups([(bank_ap(Ub), [(sv[:, kc, 1, jj * 128:(jj + 1) * 128], hT[:, kc, tsl]) for kc in range(8)])],
                              slb + [hb[tile]], [bankb[Ub]])
                    k = 2 * (it % 2)
                    it += 1
                    sg = Tf32(k)
                    sgb = [Tbuf[k], Tbuf[k + 1]]
                    act(sg, bank_ap(Gb), AF.Silu, [bankb[Gb]], sgb)
                    dve(lambda e, Ub=Ub, sg=sg, j=j, tsl=tsl: e.tensor_tensor(out=aT[:, j, tsl], in0=bank_ap(Ub), in1=sg, op=ALU.mult),
                        [bankb[Ub]] + sgb, [aTb[tile]])
                    drain(1)
            release()
    def phase_ffn_out(l, half, final):
        cb = l * NCOL
        ssb = [alloc_bank(pin=True), alloc_bank(pin=True)]
        prev_ss = None
        for oc in range(8):
            slot, slb = acquire()
            sv = slot[:, 0:2816].rearrange("p (k c) -> p k c", k=22)
            for tile in range(2):
                tsl = slice(tile * 512, (tile + 1) * 512)
                b = alloc_bank()
                mm_groups([(bank_ap(b), [(sv[:, jc, :], aT[:, jc, tsl]) for jc in range(22)])], slb + [aTb[tile]], [bankb[b]])
                if prev_ss is not None:
                    po, pt, pblk_ = prev_ss
                    mm_raw([(bank_ap(ssb[pt]), ones[:], T[:, pblk_, :], po == 0, po == 7)], [Tbuf[pblk_], constb], [bankb[ssb[pt]]])
                act(fT[:, oc, tsl], bank_ap(b), AF.Copy, [bankb[b]], [fTb[tile][oc]])
                sblk = (2 * oc + tile) % 2
                act(T[:, sblk, :], bank_ap(b), AF.Square, [bankb[b]], [Tbuf[sblk]])
                prev_ss = (oc, tile, sblk)
            release()
        po, pt, pblk_ = prev_ss
        mm_raw([(bank_ap(ssb[pt]), ones[:], T[:, pblk_, :], po == 0, po == 7)], [Tbuf[pblk_], constb], [bankb[ssb[pt]]])
        for tile in range(2):
            tsl = slice(tile * 512, (tile + 1) * 512)
            resid_update(l, half, tile, lambda oc, tsl=tsl: fT[:, oc, tsl], lambda oc, tile=tile: [fTb[tile][oc]], ssb[tile], cb + 24, final, 0)
            unpin(ssb[tile])

    for li_, l in enumerate(layers):
        S.barrier()
        layer_setup(l)
        cb = l * NCOL
        final = li_ == len(layers) - 1
        if li_ == 0:
            phase_norm(l, 0, cb + 0)
        for half in range(2):
            phase_attn(l, half)
            phase_gmlp(l, half)
            phase_pool(l, half)
            phase_merge(l, half)
            if half == 0:
                phase_norm(l, 1, cb + 0)
            else:
                phase_norm(l, 0, cb + 16)
            phase_wout(l, half)
        S.barrier()
        phase_ffn_in(l, 0)
        phase_norm(l, 1, cb + 16)
        phase_ffn_out(l, 0, final)
        phase_ffn_in(l, 1)
        if not final:
            nl = layers[li_ + 1]
            phase_norm(nl, 0, nl * NCOL + 0)
        phase_ffn_out(l, 1, final)
    drain(10 ** 6)
    assert wstate["acq"] == len(loads) and wstate["released"] == len(loads), wstate
    S.q["sp"].append(([(k, S.dmacnt[k]) for k in ("OUT0", "OUT1")], None, None, 0))

    engmap = {"pe": "tensor", "act": "scalar", "dve": "vector", "pool": "gpsimd", "sp": "sync"}
    with nc.Block() as block:
        for en in Sched.ENG:
            def body(e, en=en):
                for waits, fn, key, inc in S.q[en]:
                    for k, v in waits:
                        e.wait_ge(sems[k], v)
                    if fn is None:
                        continue
                    r = fn(e)
                    r.then_inc(sems[key], inc)
            getattr(block, engmap[en])(body)
    es.close()
    return nc


def _consts():
    k = np.arange(128)
    tri = (k[:, None] <= k[None, :]).astype(np.float32)
    pm = np.zeros((128, 12, 128), np.float32)
    s = k[:, None]
    t = k[None, :]
    for g, w in enumerate(C_WINDOWS):
        band = ((s <= t) & (s > t - w)).astype(np.float32)
        eye = (s == t).astype(np.float32)
        pm[:, g, :] = band / w - eye
        pm[:, 4 + g, :] = (s > t + 128 - w).astype(np.float32) / w
        cnt = np.minimum(t + 1, w).astype(np.float32)
        pm[:, 8 + g, :] = band / cnt - eye
    return tri, pm.reshape(128, 12 * 128)


def _pack(inputs):
    f = lambda a: np.ascontiguousarray(np.asarray(a, dtype=np.float32))
    L = 2
    cols = np.zeros((128, L * NCOL), np.float32)
    for l in range(L):
        cb = l * NCOL
        cols[:, cb + 0:cb + 8] = f(inputs["norm_mix_pre"])[l].reshape(8, 128).T
        cols[:, cb + 8:cb + 16] = f(inputs["norm_mix_post"])[l].reshape(8, 128).T
        cols[:, cb + 16:cb + 24] = f(inputs["norm_ffn_pre"])[l].reshape(8, 128).T
        cols[:, cb + 24:cb + 32] = f(inputs["norm_ffn_post"])[l].reshape(8, 128).T
        cols[:, cb + 32:cb + 36] = f(inputs["pool_scale"])[l].reshape(4, 128).T
        cols[:, cb + 36] = f(inputs["diff_subln_g"])[l]
    bc = np.concatenate([f(inputs["gmlp_norm_g"]), f(inputs["gmlp_norm_b"])], axis=1)
    lam = np.concatenate([f(inputs["lambda_q1"]), f(inputs["lambda_k1"]), f(inputs["lambda_q2"]), f(inputs["lambda_k2"])], axis=1)
    wsT = f(np.transpose(f(inputs["gmlp_w_s"]), (0, 3, 1, 2)).reshape(L, 128, 512))
    bs = f(f(inputs["gmlp_b_s"]).reshape(L, 512))
    wp = f(np.transpose(f(inputs["pool_w"]), (0, 2, 1, 3)).reshape(L, 128, 512))
    tri, pm = _consts()
    shared = {
        "w_in": f(inputs["w_in"]), "w_branch": f(inputs["w_branch"]), "w_out": f(inputs["w_out"]),
        "w_ffn_in": f(inputs["w_ffn_in"]), "w_ffn_out": f(inputs["w_ffn_out"]),
        "cols": cols, "bc": f(bc), "lam": f(lam), "wsT": wsT, "bs": bs, "wp": wp, "tri": tri, "pm": pm,
    }
    return shared


N_LAUNCH_LAYERS = [[0, 1]]


def kernel(**inputs):
    x = np.asarray(inputs["x"], dtype=np.float32)
    shared = _pack(inputs)
    cur = [np.ascontiguousarray(x[b].T) for b in range(8)]
    for layers in N_LAUNCH_LAYERS:
        nc = build_program(layers, True, True)
        in_maps = [dict(shared, xT=cur[b]) for b in range(8)]
        res = run_bass_kernel_spmd(nc, in_maps, core_ids=list(range(8)))
        cur = [np.ascontiguousarray(res.results[b]["outT"]) for b in range(8)]
    out = np.stack([c.T for c in cur], axis=0)
    return np.ascontiguousarray(out.astype(np.float32))
```

```python
import math
from contextlib import ExitStack

import numpy as np
import concourse.bass as bass
import concourse.mybir as mybir
from concourse.bass_utils import run_bass_kernel_spmd

F32 = mybir.dt.float32
BF16 = mybir.dt.bfloat16
AF = mybir.ActivationFunctionType
ALU = mybir.AluOpType

D = 1024
S_LEN = 2048
DFF = 2816
INT = 6144
EPS = 1e-6
NSLOT = 3
SLOT = 4096
NCOL = 40
IMMEDIATE_RESID = False
C_WINDOWS = (2, 4, 8, 16)


def lambda_init_fn(layer_idx):
    return 0.8 - 0.6 * math.exp(-0.3 * layer_idx)


class Buf:
    __slots__ = ("name", "w", "rs")

    def __init__(self, name):
        self.name = name
        self.w = None
        self.rs = {}


class Sched:
    ENG = ("pe", "act", "dve", "pool", "sp")

    def __init__(self):
        self.q = {e: [] for e in self.ENG}
        self.cnt = {e: 0 for e in self.ENG}
        self.seen = {e: {} for e in self.ENG}
        self.dmacnt = {}

    def _waits(self, eng, reads, writes):
        need = {}

        def add(k, v):
            if k == eng and eng == "pe":
                return
            if need.get(k, 0) < v:
                need[k] = v

        for b in reads:
            if b.w is not None:
                add(*b.w)
        for b in writes:
            if b.w is not None:
                add(*b.w)
            for k, v in b.rs.items():
                add(k, v)
        out = []
        for k, v in need.items():
            if self.seen[eng].get(k, 0) < v:
                self.seen[eng][k] = v
                out.append((k, v))
        return out

    def _commit(self, tok, reads, writes):
        k, v = tok
        for b in reads:
            if b.rs.get(k, 0) < v:
                b.rs[k] = v
        for b in writes:
            b.w = tok
            b.rs = {}

    def op(self, eng, fn, reads=(), writes=()):
        waits = self._waits(eng, reads, writes)
        self.cnt[eng] += 1
        tok = (eng, self.cnt[eng])
        self.q[eng].append((waits, fn, eng, 1))
        self._commit(tok, reads, writes)
        return tok

    def dma(self, eng, semkey, fn, reads=(), writes=()):
        waits = self._waits(eng, reads, writes)
        prev = self.dmacnt.get(semkey, 0)
        if prev > self.seen[eng].get(semkey, 0):
            self.seen[eng][semkey] = prev
            waits.append((semkey, prev))
        self.dmacnt[semkey] = prev + 16
        tok = (semkey, prev + 16)
        self.q[eng].append((waits, fn, semkey, 16))
        self._commit(tok, reads, writes)
        return tok

    def barrier(self, engs=("act", "dve")):
        snap = {e: self.cnt[e] for e in engs}
        for e in engs:
            waits = []
            for k, v in snap.items():
                if k != e and v > self.seen[e].get(k, 0):
                    self.seen[e][k] = v
                    waits.append((k, v))
            self.q[e].append((waits, None, None, 0))


def build_program(layers, first, last):
    nc = bass.Bass("TRN2", target_bir_lowering=False)
    dt_in = lambda name, shape: nc.dram_tensor(name, shape, F32, kind="ExternalInput").ap()
    xTd = dt_in("xT", [D, S_LEN])
    w_in = dt_in("w_in", [2, D, INT])
    w_branch = dt_in("w_branch", [2, 3, 512, D])
    w_out = dt_in("w_out", [2, D, D])
    w_ffn_in = dt_in("w_ffn_in", [2, D, 2 * DFF])
    w_ffn_out = dt_in("w_ffn_out", [2, DFF, D])
    colsd = dt_in("cols", [128, 2 * NCOL])
    bcd = dt_in("bc", [2, 1024])
    lamd = dt_in("lam", [2, 256])
    wsTd = dt_in("wsT", [2, 128, 512])
    bsd = dt_in("bs", [2, 512])
    wpd = dt_in("wp", [2, 128, 512])
    trid = dt_in("tri", [128, 128])
    pmd = dt_in("pm", [128, 12 * 128])
    outd = nc.dram_tensor("outT", [D, S_LEN], F32, kind="ExternalOutput").ap()

    S = Sched()
    es = ExitStack()
    sb = lambda name, shape, dt: es.enter_context(nc.sbuf_tensor(name, shape, dt))

    xT = sb("xT_s", [128, 8, S_LEN], F32)
    hT = sb("hT", [128, 8, 1024], BF16)
    U = sb("U", [128, 40960], BF16)
    WS = sb("WS", [128, NSLOT, SLOT], BF16)
    T = sb("T", [128, 8, 512], BF16)
    carry = sb("carry", [128, 512], BF16)
    cols = sb("cols_s", [128, 2 * NCOL], F32)
    gbc = sb("gbc", [128, 2, 512], F32)
    lamb = sb("lamb", [128, 256], F32)
    lamc = sb("lamc", [128, 8], F32)
    stat = sb("stat", [128, 16], F32)
    stat2 = sb("stat2", [128, 8, 10], F32)
    wsT = sb("wsT_s", [128, 4, 128], BF16)
    wpl = sb("wpl", [128, 4, 128], BF16)
    rows = sb("rows", [1, 640], BF16)
    ones = sb("ones", [128, 128], BF16)
    tri = sb("tri_s", [128, 128], BF16)
    pm = sb("pm_s", [128, 12, 128], BF16)
    ps = es.enter_context(nc.psum_tensor("ps", [128, 8, 512], F32))

    sem_names = (["pe", "act", "dve", "pool", "XIN0", "XIN1", "XIN2", "XIN3", "OUT0", "OUT1"] + [f"PS{i}" for i in range(3)]
                 + [f"PP{i}" for i in range(5)] + [f"WS{i}_{j}" for i in range(NSLOT) for j in range(3)])
    sems = {n: es.enter_context(nc.semaphore(n)) for n in sem_names}
    es.enter_context(nc.allow_low_precision("bf16 matmul operands, fp32 accumulation"))

    def uview(a, b):
        return U[:, a:b]

    KT = uview(0, 8192).rearrange("p (h t) -> p h t", h=4)
    V = uview(8192, 16384).rearrange("p (c e) -> p c e", c=16)
    yT = uview(16384, 28672).rearrange("p (c t) -> p c t", c=12)
    RQ = uview(28672, 32768).rearrange("p (c t) -> p c t", c=4)
    RMm = uview(32768, 40960).rearrange("p (c t) -> p c t", c=8)
    RMp = uview(32768, 36864).rearrange("p (c e) -> p c e", c=8)
    aT = uview(0, 22528).rearrange("p (c t) -> p c t", c=22)
    fT = uview(22528, 38912).bitcast(F32).rearrange("p (c t) -> p c t", c=8)
    oT = uview(16384, 24576).bitcast(F32).rearrange("p (c t) -> p c t", c=8)

    def Tb16(i):
        return T[:, i, :]

    def Tf32(i):
        return T[:, i:i + 2, :].rearrange("p a b -> p (a b)").bitcast(F32)

    xb = [[Buf(f"x{c}_{t}") for t in range(4)] for c in range(8)]
    hb = [Buf("h0"), Buf("h1")]
    KTb = [Buf(f"K{t}") for t in range(4)]
    Vb = [Buf(f"V{t}") for t in range(4)]
    yb = [[Buf(f"y{i}_{t}") for t in range(2)] for i in range(3)]
    RQb = Buf("RQ")
    RMb = Buf("RM")
    vallb = [Buf(f"vall{i}") for i in range(8)]
    oTb = [Buf(f"oT{i}") for i in range(8)]
    aTb = [Buf("a0"), Buf("a1")]
    fTb = [[Buf(f"f{t}_{i}") for i in range(8)] for t in range(2)]
    Tbuf = [Buf(f"T{i}") for i in range(8)]
    carryb = Buf("carry")
    constb = Buf("const")
    lpb = Buf("lp")
    statb = Buf("stat")
    bankb = [Buf(f"bank{i}") for i in range(8)]
    slotb = [[Buf(f"slot{i}_{j}") for j in range(3)] for i in range(NSLOT)]
    outb = Buf("out")

    pinned = set()
    bank_ptr = [0]

    def alloc_bank(pin=False):
        for _ in range(16):
            b = bank_ptr[0]
            bank_ptr[0] = (b + 1) % 8
            if b not in pinned:
                if pin:
                    pinned.add(b)
                return b
        raise RuntimeError("no psum bank")

    def unpin(b):
        pinned.discard(b)

    pair_ptr = [0]

    def alloc_pair(pin=False):
        for _ in range(8):
            k = pair_ptr[0]
            pair_ptr[0] = (k + 1) % 4
            if 2 * k not in pinned and 2 * k + 1 not in pinned:
                if pin:
                    pinned.add(2 * k)
                    pinned.add(2 * k + 1)
                return 2 * k
        raise RuntimeError("no psum bank pair")

    loads = []
    wstate = {"issued": 0, "released": 0, "acq": 0}

    def issue_loads():
        while wstate["issued"] < len(loads) and wstate["issued"] < wstate["released"] + NSLOT:
            m = wstate["issued"]
            s = m % NSLOT
            for i, (dst, src_) in enumerate(loads[m]):
                o = dst(WS[:, s, :])
                S.dma("pool", f"WS{s}_{i}", (lambda e, o=o, src_=src_: e.dma_start(out=o, in_=src_)), reads=(), writes=[slotb[s][i]])
            wstate["issued"] += 1

    def acquire():
        k = wstate["acq"]
        wstate["acq"] += 1
        issue_loads()
        assert wstate["issued"] > k, (k, wstate)
        s = k % NSLOT
        return WS[:, s, :], slotb[s]

    def release():
        wstate["released"] += 1
        issue_loads()

    def v3(kc):
        return lambda sl: sl.rearrange("p (k c) -> p k c", k=kc)

    def plan_loads(l):
        wi = w_in[l].rearrange("(k p) c -> p k c", p=128)
        wfi = w_ffn_in[l].rearrange("(k p) c -> p k c", p=128)
        wfo = w_ffn_out[l].rearrange("(k p) c -> p k c", p=128)
        wo = w_out[l].rearrange("(k p) c -> p k c", p=128)
        mix = []
        for c0 in (1536, 2048, 1024, 0, 512, 2560):
            mix.append([(v3(8), wi[:, :, c0:c0 + 512])])
        for dc in range(8):
            g = []
            for i in range(3):
                c0 = 3072 + i * 1024 + dc * 128
                g.append((lambda sl, i=i: sl[:, 0:3072].rearrange("p (i k c) -> p i k c", i=3, k=8)[:, i],
                          wi[:, :, c0:c0 + 128]))
            mix.append(g)
            b = []
            for i in range(3):
                wb = w_branch[l, i].rearrange("(k p) c -> p k c", p=128)
                b.append((lambda sl, i=i: sl[:, 0:1536].rearrange("p (i k c) -> p i k c", i=3, k=4)[:, i],
                          wb[:, :, dc * 128:(dc + 1) * 128]))
            mix.append(b)
        mix.append([(v3(8), wo[:, :, 0:512])])
        mix.append([(v3(8), wo[:, :, 512:1024])])
        ffn = []
        for jp in range(11):
            ffn.append([
                (lambda sl: sl.rearrange("p (k a c) -> p k a c", k=8, a=2)[:, :, 0, :], wfi[:, :, jp * 256:(jp + 1) * 256]),
                (lambda sl: sl.rearrange("p (k a c) -> p k a c", k=8, a=2)[:, :, 1, :],
                 wfi[:, :, DFF + jp * 256:DFF + (jp + 1) * 256]),
            ])
        for oc in range(8):
            ffn.append([(lambda sl: sl[:, 0:2816].rearrange("p (k c) -> p k c", k=22), wfo[:, :, oc * 128:(oc + 1) * 128])])
        return mix + mix + ffn + ffn

    for l in layers:
        loads.extend(plan_loads(l))

    def mm_groups(groups, reads, writes):
        def fn(e, groups=groups):
            ins = None
            for out_ap, pairs in groups:
                n = len(pairs)
                for i, (lh, rh) in enumerate(pairs):
                    ins = e.matmul(out_ap, lh, rh, start=(i == 0), stop=(i == n - 1))
            return ins

        return S.op("pe", fn, reads=reads, writes=writes)

    def mm_raw(items, reads, writes):
        def fn(e, items=items):
            ins = None
            for o, lh, rh, st, sp in items:
                ins = e.matmul(o, lh, rh, start=st, stop=sp)
            return ins

        return S.op("pe", fn, reads=reads, writes=writes)

    def act(out, in_, func, reads, writes, scale=1.0, bias=0.0):
        return S.op("act", lambda e: e.activation(out=out, in_=in_, func=func, scale=scale, bias=bias),
                    reads=reads, writes=writes)

    class _Rec:
        def __getattr__(self, name):
            def f(*a, **kw):
                self.call = (name, a, kw)
            return f

    def dve(fnc, reads, writes):
        r = _Rec()
        fnc(r)
        name, a, kw = r.call
        return S.op("dve", lambda e: getattr(e, name)(*a, **kw), reads=reads, writes=writes)

    def dv(method, reads, writes, **kw):
        return S.op("dve", lambda e: getattr(e, method)(**kw), reads=reads, writes=writes)

    def evac_copy(idx, out, in_, reads, writes):
        if idx % 2 == 0:
            act(out, in_, AF.Copy, reads, writes)
        else:
            dve(lambda e: e.tensor_copy(out=out, in_=in_), reads, writes)

    def bank_ap(b):
        return ps[:, b, :]

    def rstd_ops(out, bank, n, inv_n, post_mul, reads, writes):
        src_ = ps[:, bank, 0:n]
        act(out, src_, AF.Ln, reads, writes, scale=inv_n, bias=EPS)
        act(out, out, AF.Exp, writes, writes, scale=-0.5, bias=(0.0 if post_mul is None else math.log(post_mul)))

    S.dma("sp", "PS0", lambda e: e.dma_start(out=cols[:], in_=colsd), writes=[constb])
    S.dma("pool", "PP0", lambda e: e.dma_start(out=tri[:], in_=trid), writes=[constb])
    S.dma("pool", "PP1", lambda e: e.dma_start(out=pm[:].rearrange("p a b -> p (a b)"), in_=pmd), writes=[constb])
    dve(lambda e: e.memset(ones[:], 1.0), [], [constb])
    dve(lambda e: e.memset(rows[0:1, 512:640], 1.0), [], [constb])
    late_x = []
    if first:
        for t4 in range(4):
            def emit_x(t4=t4):
                src_ = xTd.rearrange("(c p) t -> p c t", p=128)[:, :, t4 * 512:(t4 + 1) * 512]
                S.dma("sp", f"XIN{t4}", (lambda e: e.dma_start(out=xT[:, :, t4 * 512:(t4 + 1) * 512], in_=src_)),
                      reads=([KTb[0]] if t4 >= 2 else []), writes=[xb[c][t4] for c in range(8)])
            if t4 < 2:
                emit_x()
            else:
                late_x.append(emit_x)

    def layer_setup(l):
        rd = [lpb]
        S.dma("sp", "PS1", lambda e: e.dma_start(out=gbc[:].rearrange("p a b -> p (a b)"), in_=bcd[l:l + 1, :].partition_broadcast(128)), writes=rd)
        S.dma("sp", "PS2", lambda e: e.dma_start(out=lamb[:], in_=lamd[l:l + 1, :].partition_broadcast(128)), writes=rd)
        S.dma("pool", "PP2", lambda e: e.dma_start(out=wsT[:].rearrange("p a b -> p (a b)"), in_=wsTd[l]), writes=rd)
        S.dma("pool", "PP3", lambda e: e.dma_start(out=wpl[:].rearrange("p a b -> p (a b)"), in_=wpd[l]), writes=rd)
        S.dma("pool", "PP4", lambda e: e.dma_start(out=rows[0:1, 0:512], in_=bsd[l:l + 1, :]), writes=rd)
        for g in range(4):
            dve(lambda e, g=g: e.tensor_tensor(out=wsT[:, g, :], in0=wsT[:, g, :], in1=tri[:], op=ALU.mult), [lpb, constb], [lpb])
        li = lambda_init_fn(l)
        tmp = Tf32(0)
        dve(lambda e: e.tensor_tensor(out=tmp[:, 0:64], in0=lamb[:, 0:64], in1=lamb[:, 64:128], op=ALU.mult), [lpb], [Tbuf[0], Tbuf[1]])
        dve(lambda e: e.tensor_tensor(out=tmp[:, 64:128], in0=lamb[:, 128:192], in1=lamb[:, 192:256], op=ALU.mult), [lpb], [Tbuf[0], Tbuf[1]])
        dve(lambda e: e.reduce_sum(out=lamc[:, 2:3], in_=tmp[:, 0:64], axis=mybir.AxisListType.X), [Tbuf[0]], [statb])
        dve(lambda e: e.reduce_sum(out=lamc[:, 3:4], in_=tmp[:, 64:128], axis=mybir.AxisListType.X), [Tbuf[0]], [statb])
        act(lamc[:, 4:6], lamc[:, 2:4], AF.Exp, [statb], [statb])
        dve(lambda e: e.tensor_tensor(out=lamc[:, 6:7], in0=lamc[:, 5:6], in1=lamc[:, 4:5], op=ALU.subtract), [statb], [statb])
        dve(lambda e: e.tensor_scalar(out=lamc[:, 0:1], in0=lamc[:, 6:7], scalar1=-li, scalar2=None, op0=ALU.add), [statb], [lpb])

    def phase_norm(l, half, colbase):
        for tile in range(2):
            tg = half * 1024 + tile * 512
            t4 = half * 2 + tile
            b = alloc_bank()
            sq = T[:, 0:4, :]
            for r in range(2):
                act(sq, xT[:, 4 * r:4 * r + 4, tg:tg + 512], AF.Square,
                    [xb[c][t4] for c in range(4 * r, 4 * r + 4)], Tbuf[0:4])
                mm_raw([(bank_ap(b), ones[:], sq[:, k, :], (r == 0 and k == 0), (r == 1 and k == 3)) for k in range(4)],
                       Tbuf[0:4] + [constb], [bankb[b]])
            blk = 4 + 2 * tile
            rs = Tf32(blk)
            rstd_ops(rs, b, 512, 1.0 / D, None, [bankb[b]], [Tbuf[blk], Tbuf[blk + 1]])
            for c in range(8):
                dve(lambda e, c=c: e.scalar_tensor_tensor(out=hT[:, c, tile * 512:(tile + 1) * 512], in0=xT[:, c, tg:tg + 512],
                                                           scalar=cols[:, colbase + c:colbase + c + 1], in1=rs,
                                                           op0=ALU.mult, op1=ALU.mult),
                    [xb[c][t4], Tbuf[blk], Tbuf[blk + 1], constb], [hb[tile]])

    def proj_feat(slot, slb, ncol, dest_fn, func, scale, writes_fn):
        sv = slot.rearrange("p (k c) -> p k c", k=8)
        n = 0
        for g in range(ncol):
            for tile in range(2):
                b = alloc_bank()
                mm_groups([(bank_ap(b), [(sv[:, kc, g * 128:(g + 1) * 128], hT[:, kc, tile * 512:(tile + 1) * 512]) for kc in range(8)])],
                          slb + [hb[tile]], [bankb[b]])
                out = dest_fn(g, tile)
                if func is None:
                    evac_copy(n, out, bank_ap(b), [bankb[b]], writes_fn(g, tile))
                else:
                    act(out, bank_ap(b), func, [bankb[b]], writes_fn(g, tile), scale=scale)
                n += 1
                drain(1)

    def proj_tok(slot, slb, chunk):
        sv = slot.rearrange("p (k c) -> p k c", k=8)
        b = alloc_bank()
        tile = chunk // 4
        mm_groups([(bank_ap(b), [(hT[:, kc, chunk * 128:(chunk + 1) * 128], sv[:, kc, :]) for kc in range(8)])],
                  slb + [hb[tile]], [bankb[b]])
        return b

    def phase_attn(l, half):
        li = lambda_init_fn(l)
        cb = l * NCOL
        slot, slb = acquire()
        proj_feat(slot, slb, 4, lambda h, tile: KT[:, h, half * 1024 + tile * 512: half * 1024 + (tile + 1) * 512], None, 1.0,
                  lambda h, tile: [KTb[half * 2 + tile]])
        release()
        while late_x:
            late_x.pop(0)()
        slot, slb = acquire()
        for chunk in range(8):
            b = proj_tok(slot, slb, chunk)
            evac_copy(chunk, V[:, half * 8 + chunk, :], bank_ap(b), [bankb[b]], [Vb[half * 2 + chunk // 4]])
            drain(1)
        release()
        drain(10 ** 6)
        slot, slb = acquire()
        proj_feat(slot, slb, 4, lambda h, tile: RQ[:, h, tile * 512:(tile + 1) * 512], AF.Copy, 0.125, lambda h, tile: [RQb])
        release()
        iters = []
        for h in range(4):
            for qt in range(4):
                gq0 = half * 8 + qt * 2
                njs = gq0 + 2
                for j in range(njs):
                    iters.append((h, qt, j, njs))
        n_it = len(iters)

        def emit_qk(i):
            h, qt, j, njs = iters[i]
            q0 = qt * 256
            gq0 = half * 8 + qt * 2
            c0 = 0 if j <= gq0 else 128
            sbk = 2 * (i % 2)
            items = []
            for c in range(2):
                items.append((ps[:, sbk + c, c0:256], KT[c * 64:(c + 1) * 64, h, j * 128:(j + 1) * 128],
                              RQ[c * 64:(c + 1) * 64, h, q0 + c0:q0 + 256], True, True))
            mm_raw(items, [KTb[j // 4], RQb], [bankb[sbk], bankb[sbk + 1]])

        pending = []
        emit_qk(0)
        pendA = []
        for i in range(n_it):
            h, qt, j, njs = iters[i]
            q0 = qt * 256
            gq0 = half * 8 + qt * 2
            c0 = 0 if j <= gq0 else 128
            sbk = 2 * (i % 2)
            Ob = 4 + 2 * ((h * 4 + qt) % 2)
            Lb = Ob + 1
            O3 = ps[:, Ob, :].rearrange("p (a q) -> p a q", a=2)
            L3 = ps[:, Lb, :].rearrange("p (a q) -> p a q", a=2)
            if i + 1 < n_it:
                emit_qk(i + 1)
            S3 = ps[:, sbk:sbk + 2, 0:256]
            pblk = i % 3
            P3 = T[:, pblk, :].rearrange("p (a q) -> p a q", a=2)
            act(P3[:, :, c0:256], S3[:, :, c0:256], AF.Exp, [bankb[sbk], bankb[sbk + 1]], [Tbuf[pblk]])
            if j >= gq0:
                for c in range(2):
                    dve(lambda e: e.tensor_tensor(out=P3[:, c, c0:c0 + 128], in0=P3[:, c, c0:c0 + 128], in1=tri[:], op=ALU.mult),
                        [Tbuf[pblk], constb], [Tbuf[pblk]])
            vj = V[:, j, h * 128:(h + 1) * 128]
            if c0 == 0:
                pv = [(O3[:, :, :], vj, P3[:, :, :], j == 0, j == njs - 1),
                      (L3[:, :, :], ones[:], P3[:, :, :], j == 0, j == njs - 1)]
            else:
                pv = []
                for c in range(2):
                    pv.append((O3[:, c, c0:256], vj, P3[:, c, c0:256], False, c == 1 and j == njs - 1))
                    pv.append((L3[:, c, c0:256], ones[:], P3[:, c, c0:256], False, c == 1 and j == njs - 1))
            mm_raw(pv, [Vb[j // 4], Tbuf[pblk], constb], [bankb[Ob], bankb[Lb]])
            if pending and i - pending[0][0] >= 8:
                pending.pop(0)[1]()
            if pendA and pendA[0][0] < i:
                pendA.pop(0)[1]()
            if j != njs - 1:
                continue

            def stage_a(i=i, h=h, qt=qt, q0=q0, Ob=Ob, Lb=Lb):
              if pending:
                pending.pop(0)[1]()
              Ocp = Tf32(3)
              Lcp = Tf32(5)
              act(Lcp, bank_ap(Lb), AF.Ln, [bankb[Lb]], [Tbuf[5], Tbuf[6]])
              act(Lcp, Lcp, AF.Exp, [Tbuf[5], Tbuf[6]], [Tbuf[5], Tbuf[6]], scale=-1.0)
              dve(lambda e: e.tensor_tensor(out=Ocp, in0=bank_ap(Ob), in1=Lcp, op=ALU.mult),
                  [bankb[Ob], Tbuf[5], Tbuf[6]], [Tbuf[3], Tbuf[4]])
              diff = Lcp[:, 0:256]
              dve(lambda e: e.scalar_tensor_tensor(out=diff, in0=Ocp[:, 256:512], scalar=lamc[:, 0:1], in1=Ocp[:, 0:256],
                                                   op0=ALU.mult, op1=ALU.add),
                  [Tbuf[3], Tbuf[4], lpb], [Tbuf[5]])
              sqd = T[:, 6, 0:256]
              dve(lambda e: e.tensor_tensor(out=sqd, in0=diff, in1=diff, op=ALU.mult), [Tbuf[5]], [Tbuf[6]])

              def stage_b(h=h, qt=qt, q0=q0, diff=diff, sqd=sqd, Ocp=Ocp):
                  ssb = 7 if False else 1
                  mm_raw([(ps[:, ssb, 256:512], ones[:], sqd, True, True)], [Tbuf[6], constb], [bankb[ssb]])
                  rstd = Ocp[:, 256:512]
                  act(rstd, ps[:, ssb, 256:512], AF.Ln, [bankb[ssb]], [Tbuf[4]], scale=1.0 / 128, bias=EPS)
                  act(rstd, rstd, AF.Exp, [Tbuf[4]], [Tbuf[4]], scale=-0.5, bias=math.log(1.0 - li))
                  dve(lambda e: e.scalar_tensor_tensor(out=yT[:, 4 + h, q0:q0 + 256], in0=diff, scalar=cols[:, cb + 36:cb + 37], in1=rstd,
                                                       op0=ALU.mult, op1=ALU.mult),
                      [Tbuf[5], Tbuf[4], constb], [yb[1][qt // 2]])

              pending.append((i, stage_b))
            pendA.append((i, stage_a))
        while pendA:
            pendA.pop(0)[1]()
        while pending:
            pending.pop(0)[1]()

    def phase_gmlp(l, half):
        slot, slb = acquire()
        proj_feat(slot, slb, 4, lambda g, tile: RQ[:, g, tile * 512:(tile + 1) * 512], AF.Gelu, 1.0, lambda g, tile: [RQb])
        release()
        slot, slb = acquire()
        vall = RMm.rearrange("p c t -> p (c t)").bitcast(F32).rearrange("p (c e) -> p c e", c=8)
        for chunk in range(8):
            b = proj_tok(slot, slb, chunk)
            act(vall[:, chunk, :], bank_ap(b), AF.Gelu, [bankb[b]], [RMb, vallb[chunk]])
            dve(lambda e: e.bn_stats(out=stat2[:, chunk, 0:6], in_=vall[:, chunk, :]), [vallb[chunk]], [statb])
            dve(lambda e: e.bn_aggr(out=stat2[:, chunk, 6:8], in_=stat2[:, chunk, 0:6]), [statb], [statb])
        release()
        act(stat2[:, :, 8:9], stat2[:, :, 7:8], AF.Sqrt, [statb], [statb], scale=1.0, bias=EPS)
        dve(lambda e: e.reciprocal(out=stat2[:, :, 8:9], in_=stat2[:, :, 8:9]), [statb], [statb])
        dve(lambda e: e.tensor_tensor(out=stat2[:, :, 9:10], in0=stat2[:, :, 6:7], in1=stat2[:, :, 8:9], op=ALU.mult), [statb], [statb])
        dve(lambda e: e.tensor_scalar(out=stat2[:, :, 9:10], in0=stat2[:, :, 9:10], scalar1=-1.0, scalar2=None, op0=ALU.mult),
            [statb], [statb])
        pslot, pslb = acquire()
        for chunk in range(8):
            vf = vall[:, chunk, :]
            vb_ = [vallb[chunk]]
            dve(lambda e: e.tensor_scalar(out=vf, in0=vf, scalar1=stat2[:, chunk, 8:9], scalar2=stat2[:, chunk, 9:10],
                                          op0=ALU.mult, op1=ALU.add), vb_ + [statb], vb_)
            dve(lambda e: e.tensor_tensor(out=vf, in0=vf, in1=gbc[:, 0, :], op=ALU.mult), vb_ + [lpb], vb_)
            nblk = 4 + (chunk % 2)
            vn = T[:, nblk, :]
            dve(lambda e: e.tensor_tensor(out=vn, in0=vf, in1=gbc[:, 1, :], op=ALU.add), vb_ + [lpb], [Tbuf[nblk]])
            mb = alloc_bank()
            items = []
            for g in range(4):
                o = ps[:, mb, g * 128:(g + 1) * 128]
                items.append((o, vn[:, g * 128:(g + 1) * 128], wsT[:, g, :], True, False))
                items.append((o, rows[0:1, 512:640], rows[0:1, g * 128:(g + 1) * 128], False, True))
            mm_raw(items, [Tbuf[nblk], lpb, constb], [bankb[mb]])
            tile = chunk // 4
            dve(lambda e: e.tensor_tensor(
                out=yT[:, 0:4, chunk * 128:(chunk + 1) * 128], in0=ps[:, mb, :].rearrange("p (g t) -> p g t", g=4),
                in1=RQ[:, :, chunk * 128:(chunk + 1) * 128], op=ALU.mult),
                [bankb[mb], RQb], [yb[0][tile]])
            pbk = proj_tok(pslot, pslb, chunk)
            act(RMp[:, chunk, :], bank_ap(pbk), AF.Copy, [bankb[pbk]], [RMb, vallb[chunk // 2]])
        release()

    def phase_pool(l, half):
        cb = l * NCOL
        for chunk in range(8):
            gchunk = half * 8 + chunk
            b = alloc_bank()
            items = []
            for g in range(4):
                o = ps[:, b, g * 128:(g + 1) * 128]
                cur = RMp[:, chunk, g * 128:(g + 1) * 128]
                if gchunk == 0:
                    items.append((o, cur, pm[:, 8 + g, :], True, True))
                else:
                    prev = carry[:, g * 128:(g + 1) * 128] if chunk == 0 else RMp[:, chunk - 1, g * 128:(g + 1) * 128]
                    items.append((o, cur, pm[:, g, :], True, False))
                    items.append((o, prev, pm[:, 4 + g, :], False, True))
            mm_raw(items, [RMb, carryb, constb], [bankb[b]])
            evac_copy(chunk, RQ[:, :, chunk * 128:(chunk + 1) * 128], ps[:, b, :].rearrange("p (g t) -> p g t", g=4),
                      [bankb[b]], [RQb])
        if half == 0:
            dve(lambda e: e.tensor_copy(out=carry[:], in_=RMp[:, 7, :]), [RMb], [carryb])
        for g in range(4):
            for tile in range(2):
                b = alloc_bank()
                mm_raw([(bank_ap(b), wpl[:, g, :], RQ[:, g, tile * 512:(tile + 1) * 512], True, True)], [lpb, RQb], [bankb[b]])
                act(yT[:, 8 + g, tile * 512:(tile + 1) * 512], bank_ap(b), AF.Copy, [bankb[b], constb], [yb[2][tile]],
                    scale=cols[:, cb + 32 + g:cb + 33 + g])

    def phase_merge(l, half):
        for dc in range(8):
            gsl, gsb = acquire()
            bsl, bsb = acquire()
            gv = gsl[:, 0:3072].rearrange("p (i k c) -> p i k c", i=3, k=8)
            bv = bsl[:, 0:1536].rearrange("p (i k c) -> p i k c", i=3, k=4)
            for tile in range(2):
                tsl = slice(tile * 512, (tile + 1) * 512)
                gb = [Tf32(0), Tf32(2)]
                gbb = [[Tbuf[0], Tbuf[1]], [Tbuf[2], Tbuf[3]]]
                tmp = Tf32(4)
                tmpb = [Tbuf[4], Tbuf[5]]
                m = Tf32(6)
                mbuf = [Tbuf[6], Tbuf[7]]
                Gbs = []
                for i in range(3):
                    Gb = alloc_bank()
                    mm_groups([(bank_ap(Gb), [(gv[:, i, kc, :], hT[:, kc, tsl]) for kc in range(8)])], gsb + [hb[tile]], [bankb[Gb]])
                    Gbs.append(Gb)
                for i in range(3):
                    Gb = Gbs[i]
                    Ub = alloc_bank()
                    mm_groups([(bank_ap(Ub), [(bv[:, i, wc, :], yT[:, i * 4 + wc, tsl]) for wc in range(4)])],
                              bsb + [yb[i][tile]], [bankb[Ub]])
                    k = i % 2
                    act(gb[k], bank_ap(Gb), AF.Sigmoid, [bankb[Gb]], gbb[k])
                    if i == 0:
                        dve(lambda e: e.tensor_tensor(out=m, in0=bank_ap(Ub), in1=gb[k], op=ALU.mult),
                            [bankb[Ub]] + gbb[k], mbuf)
                    else:
                        dve(lambda e: e.tensor_tensor(out=tmp, in0=bank_ap(Ub), in1=gb[k], op=ALU.mult),
                            [bankb[Ub]] + gbb[k], tmpb)
                        if i == 1:
                            dve(lambda e: e.tensor_tensor(out=m, in0=m, in1=tmp, op=ALU.add), mbuf + tmpb, mbuf)
                        else:
                            dve(lambda e: e.tensor_tensor(out=RMm[:, dc, tsl], in0=m, in1=tmp, op=ALU.add),
                                mbuf + tmpb, [RMb] + vallb)
            release()
            release()

    bg = []

    def drain(n=1):
        for _ in range(n):
            if bg:
                bg.pop(0)()

    def resid_update(l, half, tile, src_fn, srcbufs, ssb, colbase, final, tmpB):
        tg = half * 1024 + tile * 512
        t4 = half * 2 + tile
        blk = 4 + 2 * tile
        rs = Tf32(blk)
        rstd_ops(rs, ssb, 512, 1.0 / D, None, [bankb[ssb]], [Tbuf[blk], Tbuf[blk + 1]])

        def unit(oc):
            sa = src_fn(oc)
            dve(lambda e: e.tensor_tensor(out=sa, in0=sa, in1=rs, op=ALU.mult),
                srcbufs(oc) + [Tbuf[blk], Tbuf[blk + 1]], srcbufs(oc))
            dve(lambda e: e.scalar_tensor_tensor(out=xT[:, oc, tg:tg + 512], in0=sa, scalar=cols[:, colbase + oc:colbase + oc + 1],
                                                 in1=xT[:, oc, tg:tg + 512], op0=ALU.mult, op1=ALU.add),
                srcbufs(oc) + [constb], [xb[oc][t4]])

        for oc in range(8):
            bg.append(lambda oc=oc: unit(oc))
        if final:
            def out_unit():
                dst = outd.rearrange("(c p) t -> p c t", p=128)[:, :, tg:tg + 512]
                S.dma("sp", f"OUT{tile}", lambda e: e.dma_start(out=dst, in_=xT[:, :, tg:tg + 512]),
                      reads=[xb[c][t4] for c in range(8)], writes=[outb])
            bg.append(out_unit)
        if IMMEDIATE_RESID:
            drain(10 ** 6)

    def phase_wout(l, half):
        cb = l * NCOL
        sA, sAb = acquire()
        sB, sBb = acquire()
        vA = sA.rearrange("p (k c) -> p k c", k=8)
        vB = sB.rearrange("p (k c) -> p k c", k=8)
        ybufs = [yb[0][0], yb[0][1], yb[1][0], yb[1][1]]
        for tile in range(2):
            tsl = slice(tile * 512, (tile + 1) * 512)
            ssb = alloc_bank(pin=True)
            for oc in range(8):
                wv = vA if oc < 4 else vB
                wb = sAb if oc < 4 else sBb
                b = alloc_bank()
                mm_groups([(bank_ap(b), [(wv[:, kc, (oc % 4) * 128:(oc % 4 + 1) * 128], RMm[:, kc, tsl]) for kc in range(8)])],
                          wb + [RMb], [bankb[b]])
                if oc > 0:
                    pb = (oc - 1) % 2
                    mm_raw([(bank_ap(ssb), ones[:], T[:, pb, :], oc - 1 == 0, False)], [Tbuf[pb], constb], [bankb[ssb]])
                act(oT[:, oc, :], bank_ap(b), AF.Copy, [bankb[b]], [oTb[oc]] + ybufs)
                sblk = oc % 2
                act(T[:, sblk, :], bank_ap(b), AF.Square, [bankb[b]], [Tbuf[sblk]])
            mm_raw([(bank_ap(ssb), ones[:], T[:, 7 % 2, :], False, True)], [Tbuf[7 % 2], constb], [bankb[ssb]])
            resid_update(l, half, tile, lambda oc: oT[:, oc, :], lambda oc: [oTb[oc]], ssb, cb + 8, False, 6 if tile == 0 else 4)
            if tile == 0:
                drain(10 ** 6)
            unpin(ssb)
        release()
        release()

    def phase_ffn_in(l, half):
        it = 0
        for jp in range(11):
            if jp == 4:
                drain(10 ** 6)
            slot, slb = acquire()
            sv = slot.rearrange("p (k a c) -> p k a c", k=8, a=2)
            for jj in range(2):
                j = 2 * jp + jj
                for tile in range(2):
                    tsl = slice(tile * 512, (tile + 1) * 512)
                    Gb = alloc_bank()
                    mm_groups([(bank_ap(Gb), [(sv[:, kc, 0, jj * 128:(jj + 1) * 128], hT[:, kc, tsl]) for kc in range(8)])],
                              slb + [hb[tile]], [bankb[Gb]])
                    Ub = alloc_bank()
                    mm_groups([(bank_ap(Ub), [(sv[:, kc, 1, jj * 128:(jj + 1) * 128], hT[:, kc, tsl]) for kc in range(8)])],
                              slb + [hb[tile]], [bankb[Ub]])
                    k = 2 * (it % 2)
                    it += 1
                    sg = Tf32(k)
                    sgb = [Tbuf[k], Tbuf[k + 1]]
                    act(sg, bank_ap(Gb), AF.Silu, [bankb[Gb]], sgb)
                    dve(lambda e, Ub=Ub, sg=sg, j=j, tsl=tsl: e.tensor_tensor(out=aT[:, j, tsl], in0=bank_ap(Ub), in1=sg, op=ALU.mult),
                        [bankb[Ub]] + sgb, [aTb[tile]])
                    drain(1)
            release()
    def phase_ffn_out(l, half, final):
        cb = l * NCOL
        ssb = [alloc_bank(pin=True), alloc_bank(pin=True)]
        prev_ss = None
        for oc in range(8):
            slot, slb = acquire()
            sv = slot[:, 0:2816].rearrange("p (k c) -> p k c", k=22)
            for tile in range(2):
                tsl = slice(tile * 512, (tile + 1) * 512)
                b = alloc_bank()
                mm_groups([(bank_ap(b), [(sv[:, jc, :], aT[:, jc, tsl]) for jc in range(22)])], slb + [aTb[tile]], [bankb[b]])
                if prev_ss is not None:
                    po, pt, pblk_ = prev_ss
                    mm_raw([(bank_ap(ssb[pt]), ones[:], T[:, pblk_, :], po == 0, po == 7)], [Tbuf[pblk_], constb], [bankb[ssb[pt]]])
                act(fT[:, oc, tsl], bank_ap(b), AF.Copy, [bankb[b]], [fTb[tile][oc]])
                sblk = (2 * oc + tile) % 2
                act(T[:, sblk, :], bank_ap(b), AF.Square, [bankb[b]], [Tbuf[sblk]])
                prev_ss = (oc, tile, sblk)
            release()
        po, pt, pblk_ = prev_ss
        mm_raw([(bank_ap(ssb[pt]), ones[:], T[:, pblk_, :], po == 0, po == 7)], [Tbuf[pblk_], constb], [bankb[ssb[pt]]])
        for tile in range(2):
            tsl = slice(tile * 512, (tile + 1) * 512)
            resid_update(l, half, tile, lambda oc, tsl=tsl: fT[:, oc, tsl], lambda oc, tile=tile: [fTb[tile][oc]], ssb[tile], cb + 24, final, 0)
            unpin(ssb[tile])

    for li_, l in enumerate(layers):
        S.barrier()
        layer_setup(l)
        cb = l * NCOL
        final = li_ == len(layers) - 1
        if li_ == 0:
            phase_norm(l, 0, cb + 0)
        for half in range(2):
            phase_attn(l, half)
            phase_gmlp(l, half)
            phase_pool(l, half)
            phase_merge(l, half)
            if half == 0:
                phase_norm(l, 1, cb + 0)
            else:
                phase_norm(l, 0, cb + 16)
            phase_wout(l, half)
        S.barrier()
        phase_ffn_in(l, 0)
        phase_norm(l, 1, cb + 16)
        phase_ffn_out(l, 0, final)
        phase_ffn_in(l, 1)
        if not final:
            nl = layers[li_ + 1]
            phase_norm(nl, 0, nl * NCOL + 0)
        phase_ffn_out(l, 1, final)
    drain(10 ** 6)
    assert wstate["acq"] == len(loads) and wstate["released"] == len(loads), wstate
    S.q["sp"].append(([(k, S.dmacnt[k]) for k in ("OUT0", "OUT1")], None, None, 0))

    engmap = {"pe": "tensor", "act": "scalar", "dve": "vector", "pool": "gpsimd", "sp": "sync"}
    with nc.Block() as block:
        for en in Sched.ENG:
            def body(e, en=en):
                for waits, fn, key, inc in S.q[en]:
                    for k, v in waits:
                        e.wait_ge(sems[k], v)
                    if fn is None:
                        continue
                    r = fn(e)
                    r.then_inc(sems[key], inc)
            getattr(block, engmap[en])(body)
    es.close()
    return nc


def _consts():
    k = np.arange(128)
    tri = (k[:, None] <= k[None, :]).astype(np.float32)
    pm = np.zeros((128, 12, 128), np.float32)
    s = k[:, None]
    t = k[None, :]
    for g, w in enumerate(C_WINDOWS):
        band = ((s <= t) & (s > t - w)).astype(np.float32)
        eye = (s == t).astype(np.float32)
        pm[:, g, :] = band / w - eye
        pm[:, 4 + g, :] = (s > t + 128 - w).astype(np.float32) / w
        cnt = np.minimum(t + 1, w).astype(np.float32)
        pm[:, 8 + g, :] = band / cnt - eye
    return tri, pm.reshape(128, 12 * 128)


def _pack(inputs):
    f = lambda a: np.ascontiguousarray(np.asarray(a, dtype=np.float32))
    L = 2
    cols = np.zeros((128, L * NCOL), np.float32)
    for l in range(L):
        cb = l * NCOL
        cols[:, cb + 0:cb + 8] = f(inputs["norm_mix_pre"])[l].reshape(8, 128).T
        cols[:, cb + 8:cb + 16] = f(inputs["norm_mix_post"])[l].reshape(8, 128).T
        cols[:, cb + 16:cb + 24] = f(inputs["norm_ffn_pre"])[l].reshape(8, 128).T
        cols[:, cb + 24:cb + 32] = f(inputs["norm_ffn_post"])[l].reshape(8, 128).T
        cols[:, cb + 32:cb + 36] = f(inputs["pool_scale"])[l].reshape(4, 128).T
        cols[:, cb + 36] = f(inputs["diff_subln_g"])[l]
    bc = np.concatenate([f(inputs["gmlp_norm_g"]), f(inputs["gmlp_norm_b"])], axis=1)
    lam = np.concatenate([f(inputs["lambda_q1"]), f(inputs["lambda_k1"]), f(inputs["lambda_q2"]), f(inputs["lambda_k2"])], axis=1)
    wsT = f(np.transpose(f(inputs["gmlp_w_s"]), (0, 3, 1, 2)).reshape(L, 128, 512))
    bs = f(f(inputs["gmlp_b_s"]).reshape(L, 512))
    wp = f(np.transpose(f(inputs["pool_w"]), (0, 2, 1, 3)).reshape(L, 128, 512))
    tri, pm = _consts()
    shared = {
        "w_in": f(inputs["w_in"]), "w_branch": f(inputs["w_branch"]), "w_out": f(inputs["w_out"]),
        "w_ffn_in": f(inputs["w_ffn_in"]), "w_ffn_out": f(inputs["w_ffn_out"]),
        "cols": cols, "bc": f(bc), "lam": f(lam), "wsT": wsT, "bs": bs, "wp": wp, "tri": tri, "pm": pm,
    }
    return shared


N_LAUNCH_LAYERS = [[0, 1]]


def kernel(**inputs):
    x = np.asarray(inputs["x"], dtype=np.float32)
    shared = _pack(inputs)
    cur = [np.ascontiguousarray(x[b].T) for b in range(8)]
    for layers in N_LAUNCH_LAYERS:
        nc = build_program(layers, True, True)
        in_maps = [dict(shared, xT=cur[b]) for b in range(8)]
        res = run_bass_kernel_spmd(nc, in_maps, core_ids=list(range(8)))
        cur = [np.ascontiguousarray(res.results[b]["outT"]) for b in range(8)]
    out = np.stack([c.T for c in cur], axis=0)
    return np.ascontiguousarray(out.astype(np.float32))
```

```python
import math
from contextlib import ExitStack

import numpy as np
import concourse.bass as bass
import concourse.mybir as mybir
from concourse.bass_utils import run_bass_kernel_spmd

F32 = mybir.dt.float32
BF16 = mybir.dt.bfloat16
AF = mybir.ActivationFunctionType
ALU = mybir.AluOpType

D = 1024
S_LEN = 2048
DFF = 2816
INT = 6144
EPS = 1e-6
NSLOT = 3
SLOT = 4096
NCOL = 40
IMMEDIATE_RESID = False
C_WINDOWS = (2, 4, 8, 16)


def lambda_init_fn(layer_idx):
    return 0.8 - 0.6 * math.exp(-0.3 * layer_idx)


class Buf:
    __slots__ = ("name", "w", "rs")

    def __init__(self, name):
        self.name = name
        self.w = None
        self.rs = {}


class Sched:
    ENG = ("pe", "act", "dve", "pool", "sp")

    def __init__(self):
        self.q = {e: [] for e in self.ENG}
        self.cnt = {e: 0 for e in self.ENG}
        self.seen = {e: {} for e in self.ENG}
        self.dmacnt = {}

    def _waits(self, eng, reads, writes):
        need = {}

        def add(k, v):
            if k == eng and eng == "pe":
                return
            if need.get(k, 0) < v:
                need[k] = v

        for b in reads:
            if b.w is not None:
                add(*b.w)
        for b in writes:
            if b.w is not None:
                add(*b.w)
            for k, v in b.rs.items():
                add(k, v)
        out = []
        for k, v in need.items():
            if self.seen[eng].get(k, 0) < v:
                self.seen[eng][k] = v
                out.append((k, v))
        return out

    def _commit(self, tok, reads, writes):
        k, v = tok
        for b in reads:
            if b.rs.get(k, 0) < v:
                b.rs[k] = v
        for b in writes:
            b.w = tok
            b.rs = {}

    def op(self, eng, fn, reads=(), writes=()):
        waits = self._waits(eng, reads, writes)
        self.cnt[eng] += 1
        tok = (eng, self.cnt[eng])
        self.q[eng].append((waits, fn, eng, 1))
        self._commit(tok, reads, writes)
        return tok

    def dma(self, eng, semkey, fn, reads=(), writes=()):
        waits = self._waits(eng, reads, writes)
        prev = self.dmacnt.get(semkey, 0)
        if prev > self.seen[eng].get(semkey, 0):
            self.seen[eng][semkey] = prev
            waits.append((semkey, prev))
        self.dmacnt[semkey] = prev + 16
        tok = (semkey, prev + 16)
        self.q[eng].append((waits, fn, semkey, 16))
        self._commit(tok, reads, writes)
        return tok

    def barrier(self, engs=("act", "dve")):
        snap = {e: self.cnt[e] for e in engs}
        for e in engs:
            waits = []
            for k, v in snap.items():
                if k != e and v > self.seen[e].get(k, 0):
                    self.seen[e][k] = v
                    waits.append((k, v))
            self.q[e].append((waits, None, None, 0))


def build_program(layers, first, last):
    nc = bass.Bass("TRN2", target_bir_lowering=False)
    dt_in = lambda name, shape: nc.dram_tensor(name, shape, F32, kind="ExternalInput").ap()
    xTd = dt_in("xT", [D, S_LEN])
    w_in = dt_in("w_in", [2, D, INT])
    w_branch = dt_in("w_branch", [2, 3, 512, D])
    w_out = dt_in("w_out", [2, D, D])
    w_ffn_in = dt_in("w_ffn_in", [2, D, 2 * DFF])
    w_ffn_out = dt_in("w_ffn_out", [2, DFF, D])
    colsd = dt_in("cols", [128, 2 * NCOL])
    bcd = dt_in("bc", [2, 1024])
    lamd = dt_in("lam", [2, 256])
    wsTd = dt_in("wsT", [2, 128, 512])
    bsd = dt_in("bs", [2, 512])
    wpd = dt_in("wp", [2, 128, 512])
    trid = dt_in("tri", [128, 128])
    pmd = dt_in("pm", [128, 12 * 128])
    outd = nc.dram_tensor("outT", [D, S_LEN], F32, kind="ExternalOutput").ap()

    S = Sched()
    es = ExitStack()
    sb = lambda name, shape, dt: es.enter_context(nc.sbuf_tensor(name, shape, dt))

    xT = sb("xT_s", [128, 8, S_LEN], F32)
    hT = sb("hT", [128, 8, 1024], BF16)
    U = sb("U", [128, 40960], BF16)
    WS = sb("WS", [128, NSLOT, SLOT], BF16)
    T = sb("T", [128, 8, 512], BF16)
    carry = sb("carry", [128, 512], BF16)
    cols = sb("cols_s", [128, 2 * NCOL], F32)
    gbc = sb("gbc", [128, 2, 512], F32)
    lamb = sb("lamb", [128, 256], F32)
    lamc = sb("lamc", [128, 8], F32)
    stat = sb("stat", [128, 16], F32)
    stat2 = sb("stat2", [128, 8, 10], F32)
    wsT = sb("wsT_s", [128, 4, 128], BF16)
    wpl = sb("wpl", [128, 4, 128], BF16)
    rows = sb("rows", [1, 640], BF16)
    ones = sb("ones", [128, 128], BF16)
    tri = sb("tri_s", [128, 128], BF16)
    pm = sb("pm_s", [128, 12, 128], BF16)
    ps = es.enter_context(nc.psum_tensor("ps", [128, 8, 512], F32))

    sem_names = (["pe", "act", "dve", "pool", "XIN0", "XIN1", "XIN2", "XIN3", "OUT0", "OUT1"] + [f"PS{i}" for i in range(3)]
                 + [f"PP{i}" for i in range(5)] + [f"WS{i}_{j}" for i in range(NSLOT) for j in range(3)])
    sems = {n: es.enter_context(nc.semaphore(n)) for n in sem_names}
    es.enter_context(nc.allow_low_precision("bf16 matmul operands, fp32 accumulation"))

    def uview(a, b):
        return U[:, a:b]

    KT = uview(0, 8192).rearrange("p (h t) -> p h t", h=4)
    V = uview(8192, 16384).rearrange("p (c e) -> p c e", c=16)
    yT = uview(16384, 28672).rearrange("p (c t) -> p c t", c=12)
    RQ = uview(28672, 32768).rearrange("p (c t) -> p c t", c=4)
    RMm = uview(32768, 40960).rearrange("p (c t) -> p c t", c=8)
    RMp = uview(32768, 36864).rearrange("p (c e) -> p c e", c=8)
    aT = uview(0, 22528).rearrange("p (c t) -> p c t", c=22)
    fT = uview(22528, 38912).bitcast(F32).rearrange("p (c t) -> p c t", c=8)
    oT = uview(16384, 24576).bitcast(F32).rearrange("p (c t) -> p c t", c=8)

    def Tb16(i):
        return T[:, i, :]

    def Tf32(i):
        return T[:, i:i + 2, :].rearrange("p a b -> p (a b)").bitcast(F32)

    xb = [[Buf(f"x{c}_{t}") for t in range(4)] for c in range(8)]
    hb = [Buf("h0"), Buf("h1")]
    KTb = [Buf(f"K{t}") for t in range(4)]
    Vb = [Buf(f"V{t}") for t in range(4)]
    yb = [[Buf(f"y{i}_{t}") for t in range(2)] for i in range(3)]
    RQb = Buf("RQ")
    RMb = Buf("RM")
    vallb = [Buf(f"vall{i}") for i in range(8)]
    oTb = [Buf(f"oT{i}") for i in range(8)]
    aTb = [Buf("a0"), Buf("a1")]
    fTb = [[Buf(f"f{t}_{i}") for i in range(8)] for t in range(2)]
    Tbuf = [Buf(f"T{i}") for i in range(8)]
    carryb = Buf("carry")
    constb = Buf("const")
    lpb = Buf("lp")
    statb = Buf("stat")
    bankb = [Buf(f"bank{i}") for i in range(8)]
    slotb = [[Buf(f"slot{i}_{j}") for j in range(3)] for i in range(NSLOT)]
    outb = Buf("out")

    pinned = set()
    bank_ptr = [0]

    def alloc_bank(pin=False):
        for _ in range(16):
            b = bank_ptr[0]
            bank_ptr[0] = (b + 1) % 8
            if b not in pinned:
                if pin:
                    pinned.add(b)
                return b
        raise RuntimeError("no psum bank")

    def unpin(b):
        pinned.discard(b)

    pair_ptr = [0]

    def alloc_pair(pin=False):
        for _ in range(8):
            k = pair_ptr[0]
            pair_ptr[0] = (k + 1) % 4
            if 2 * k not in pinned and 2 * k + 1 not in pinned:
                if pin:
                    pinned.add(2 * k)
                    pinned.add(2 * k + 1)
                return 2 * k
        raise RuntimeError("no psum bank pair")

    loads = []
    wstate = {"issued": 0, "released": 0, "acq": 0}

    def issue_loads():
        while wstate["issued"] < len(loads) and wstate["issued"] < wstate["released"] + NSLOT:
            m = wstate["issued"]
            s = m % NSLOT
            for i, (dst, src_) in enumerate(loads[m]):
                o = dst(WS[:, s, :])
                S.dma("pool", f"WS{s}_{i}", (lambda e, o=o, src_=src_: e.dma_start(out=o, in_=src_)), reads=(), writes=[slotb[s][i]])
            wstate["issued"] += 1

    def acquire():
        k = wstate["acq"]
        wstate["acq"] += 1
        issue_loads()
        assert wstate["issued"] > k, (k, wstate)
        s = k % NSLOT
        return WS[:, s, :], slotb[s]

    def release():
        wstate["released"] += 1
        issue_loads()

    def v3(kc):
        return lambda sl: sl.rearrange("p (k c) -> p k c", k=kc)

    def plan_loads(l):
        wi = w_in[l].rearrange("(k p) c -> p k c", p=128)
        wfi = w_ffn_in[l].rearrange("(k p) c -> p k c", p=128)
        wfo = w_ffn_out[l].rearrange("(k p) c -> p k c", p=128)
        wo = w_out[l].rearrange("(k p) c -> p k c", p=128)
        mix = []
        for c0 in (1536, 2048, 1024, 512, 0, 2560):
            mix.append([(v3(8), wi[:, :, c0:c0 + 512])])
        for dc in range(8):
            g = []
            for i in range(3):
                c0 = 3072 + i * 1024 + dc * 128
                g.append((lambda sl, i=i: sl[:, 0:3072].rearrange("p (i k c) -> p i k c", i=3, k=8)[:, i],
                          wi[:, :, c0:c0 + 128]))
            mix.append(g)
            b = []
            for i in range(3):
                wb = w_branch[l, i].rearrange("(k p) c -> p k c", p=128)
                b.append((lambda sl, i=i: sl[:, 0:1536].rearrange("p (i k c) -> p i k c", i=3, k=4)[:, i],
                          wb[:, :, dc * 128:(dc + 1) * 128]))
            mix.append(b)
        mix.append([(v3(8), wo[:, :, 0:512])])
        mix.append([(v3(8), wo[:, :, 512:1024])])
        ffn = []
        for jp in range(11):
            ffn.append([
                (lambda sl: sl.rearrange("p (k a c) -> p k a c", k=8, a=2)[:, :, 0, :], wfi[:, :, jp * 256:(jp + 1) * 256]),
                (lambda sl: sl.rearrange("p (k a c) -> p k a c", k=8, a=2)[:, :, 1, :],
                 wfi[:, :, DFF + jp * 256:DFF + (jp + 1) * 256]),
            ])
        for oc in range(8):
            ffn.append([(lambda sl: sl[:, 0:2816].rearrange("p (k c) -> p k c", k=22), wfo[:, :, oc * 128:(oc + 1) * 128])])
        return mix + mix + ffn + ffn

    for l in layers:
        loads.extend(plan_loads(l))

    def mm_groups(groups, reads, writes):
        def fn(e, groups=groups):
            ins = None
            for out_ap, pairs in groups:
                n = len(pairs)
                for i, (lh, rh) in enumerate(pairs):
                    ins = e.matmul(out_ap, lh, rh, start=(i == 0), stop=(i == n - 1))
            return ins

        return S.op("pe", fn, reads=reads, writes=writes)

    def mm_raw(items, reads, writes):
        def fn(e, items=items):
            ins = None
            for o, lh, rh, st, sp in items:
                ins = e.matmul(o, lh, rh, start=st, stop=sp)
            return ins

        return S.op("pe", fn, reads=reads, writes=writes)

    def act(out, in_, func, reads, writes, scale=1.0, bias=0.0):
        return S.op("act", lambda e: e.activation(out=out, in_=in_, func=func, scale=scale, bias=bias),
                    reads=reads, writes=writes)

    class _Rec:
        def __getattr__(self, name):
            def f(*a, **kw):
                self.call = (name, a, kw)
            return f

    def dve(fnc, reads, writes):
        r = _Rec()
        fnc(r)
        name, a, kw = r.call
        return S.op("dve", lambda e: getattr(e, name)(*a, **kw), reads=reads, writes=writes)

    def dv(method, reads, writes, **kw):
        return S.op("dve", lambda e: getattr(e, method)(**kw), reads=reads, writes=writes)

    def evac_copy(idx, out, in_, reads, writes):
        if idx % 2 == 0:
            act(out, in_, AF.Copy, reads, writes)
        else:
            dve(lambda e: e.tensor_copy(out=out, in_=in_), reads, writes)

    def bank_ap(b):
        return ps[:, b, :]

    def rstd_ops(out, bank, n, inv_n, post_mul, reads, writes):
        src_ = ps[:, bank, 0:n]
        act(out, src_, AF.Ln, reads, writes, scale=inv_n, bias=EPS)
        act(out, out, AF.Exp, writes, writes, scale=-0.5, bias=(0.0 if post_mul is None else math.log(post_mul)))

    S.dma("sp", "PS0", lambda e: e.dma_start(out=cols[:], in_=colsd), writes=[constb])
    S.dma("pool", "PP0", lambda e: e.dma_start(out=tri[:], in_=trid), writes=[constb])
    S.dma("pool", "PP1", lambda e: e.dma_start(out=pm[:].rearrange("p a b -> p (a b)"), in_=pmd), writes=[constb])
    dve(lambda e: e.memset(ones[:], 1.0), [], [constb])
    dve(lambda e: e.memset(rows[0:1, 512:640], 1.0), [], [constb])
    late_x = []
    if first:
        for t4 in range(4):
            def emit_x(t4=t4):
                src_ = xTd.rearrange("(c p) t -> p c t", p=128)[:, :, t4 * 512:(t4 + 1) * 512]
                S.dma("sp", f"XIN{t4}", (lambda e: e.dma_start(out=xT[:, :, t4 * 512:(t4 + 1) * 512], in_=src_)),
                      reads=([KTb[0]] if t4 >= 2 else []), writes=[xb[c][t4] for c in range(8)])
            if t4 < 2:
                emit_x()
            else:
                late_x.append(emit_x)

    def layer_setup(l):
        rd = [lpb]
        S.dma("sp", "PS1", lambda e: e.dma_start(out=gbc[:].rearrange("p a b -> p (a b)"), in_=bcd[l:l + 1, :].partition_broadcast(128)), writes=rd)
        S.dma("sp", "PS2", lambda e: e.dma_start(out=lamb[:], in_=lamd[l:l + 1, :].partition_broadcast(128)), writes=rd)
        S.dma("pool", "PP2", lambda e: e.dma_start(out=wsT[:].rearrange("p a b -> p (a b)"), in_=wsTd[l]), writes=rd)
        S.dma("pool", "PP3", lambda e: e.dma_start(out=wpl[:].rearrange("p a b -> p (a b)"), in_=wpd[l]), writes=rd)
        S.dma("pool", "PP4", lambda e: e.dma_start(out=rows[0:1, 0:512], in_=bsd[l:l + 1, :]), writes=rd)
        for g in range(4):
            dve(lambda e, g=g: e.tensor_tensor(out=wsT[:, g, :], in0=wsT[:, g, :], in1=tri[:], op=ALU.mult), [lpb, constb], [lpb])
        li = lambda_init_fn(l)
        tmp = Tf32(0)
        dve(lambda e: e.tensor_tensor(out=tmp[:, 0:64], in0=lamb[:, 0:64], in1=lamb[:, 64:128], op=ALU.mult), [lpb], [Tbuf[0], Tbuf[1]])
        dve(lambda e: e.tensor_tensor(out=tmp[:, 64:128], in0=lamb[:, 128:192], in1=lamb[:, 192:256], op=ALU.mult), [lpb], [Tbuf[0], Tbuf[1]])
        dve(lambda e: e.reduce_sum(out=lamc[:, 2:3], in_=tmp[:, 0:64], axis=mybir.AxisListType.X), [Tbuf[0]], [statb])
        dve(lambda e: e.reduce_sum(out=lamc[:, 3:4], in_=tmp[:, 64:128], axis=mybir.AxisListType.X), [Tbuf[0]], [statb])
        act(lamc[:, 4:6], lamc[:, 2:4], AF.Exp, [statb], [statb])
        dve(lambda e: e.tensor_tensor(out=lamc[:, 6:7], in0=lamc[:, 5:6], in1=lamc[:, 4:5], op=ALU.subtract), [statb], [statb])
        dve(lambda e: e.tensor_scalar(out=lamc[:, 0:1], in0=lamc[:, 6:7], scalar1=-li, scalar2=None, op0=ALU.add), [statb], [lpb])

    def phase_norm(l, half, colbase):
        for tile in range(2):
            tg = half * 1024 + tile * 512
            t4 = half * 2 + tile
            b = alloc_bank()
            sq = T[:, 0:4, :]
            for r in range(2):
                act(sq, xT[:, 4 * r:4 * r + 4, tg:tg + 512], AF.Square,
                    [xb[c][t4] for c in range(4 * r, 4 * r + 4)], Tbuf[0:4])
                mm_raw([(bank_ap(b), ones[:], sq[:, k, :], (r == 0 and k == 0), (r == 1 and k == 3)) for k in range(4)],
                       Tbuf[0:4] + [constb], [bankb[b]])
            blk = 4 + 2 * tile
            rs = Tf32(blk)
            rstd_ops(rs, b, 512, 1.0 / D, None, [bankb[b]], [Tbuf[blk], Tbuf[blk + 1]])
            for c in range(8):
                dve(lambda e, c=c: e.scalar_tensor_tensor(out=hT[:, c, tile * 512:(tile + 1) * 512], in0=xT[:, c, tg:tg + 512],
                                                           scalar=cols[:, colbase + c:colbase + c + 1], in1=rs,
                                                           op0=ALU.mult, op1=ALU.mult),
                    [xb[c][t4], Tbuf[blk], Tbuf[blk + 1], constb], [hb[tile]])

    def proj_feat(slot, slb, ncol, dest_fn, func, scale, writes_fn):
        sv = slot.rearrange("p (k c) -> p k c", k=8)
        n = 0
        for g in range(ncol):
            for tile in range(2):
                b = alloc_bank()
                mm_groups([(bank_ap(b), [(sv[:, kc, g * 128:(g + 1) * 128], hT[:, kc, tile * 512:(tile + 1) * 512]) for kc in range(8)])],
                          slb + [hb[tile]], [bankb[b]])
                out = dest_fn(g, tile)
                if func is None:
                    evac_copy(n, out, bank_ap(b), [bankb[b]], writes_fn(g, tile))
                else:
                    act(out, bank_ap(b), func, [bankb[b]], writes_fn(g, tile), scale=scale)
                n += 1
                drain(1)

    def proj_tok(slot, slb, chunk):
        sv = slot.rearrange("p (k c) -> p k c", k=8)
        b = alloc_bank()
        tile = chunk // 4
        mm_groups([(bank_ap(b), [(hT[:, kc, chunk * 128:(chunk + 1) * 128], sv[:, kc, :]) for kc in range(8)])],
                  slb + [hb[tile]], [bankb[b]])
        return b

    def phase_attn(l, half):
        li = lambda_init_fn(l)
        cb = l * NCOL
        slot, slb = acquire()
        proj_feat(slot, slb, 4, lambda h, tile: KT[:, h, half * 1024 + tile * 512: half * 1024 + (tile + 1) * 512], None, 1.0,
                  lambda h, tile: [KTb[half * 2 + tile]])
        release()
        while late_x:
            late_x.pop(0)()
        slot, slb = acquire()
        for chunk in range(8):
            b = proj_tok(slot, slb, chunk)
            evac_copy(chunk, V[:, half * 8 + chunk, :], bank_ap(b), [bankb[b]], [Vb[half * 2 + chunk // 4]])
            drain(1)
        release()
        drain(10 ** 6)
        slot, slb = acquire()
        proj_feat(slot, slb, 4, lambda h, tile: RQ[:, h, tile * 512:(tile + 1) * 512], AF.Copy, 0.125, lambda h, tile: [RQb])
        release()
        iters = []
        for h in range(4):
            for qt in range(4):
                gq0 = half * 8 + qt * 2
                njs = gq0 + 2
                for j in range(njs):
                    iters.append((h, qt, j, njs))
        n_it = len(iters)

        def emit_qk(i):
            h, qt, j, njs = iters[i]
            q0 = qt * 256
            gq0 = half * 8 + qt * 2
            c0 = 0 if j <= gq0 else 128
            sbk = 2 * (i % 2)
            items = []
            for c in range(2):
                items.append((ps[:, sbk + c, c0:256], KT[c * 64:(c + 1) * 64, h, j * 128:(j + 1) * 128],
                              RQ[c * 64:(c + 1) * 64, h, q0 + c0:q0 + 256], True, True))
            mm_raw(items, [KTb[j // 4], RQb], [bankb[sbk], bankb[sbk + 1]])

        pending = []
        emit_qk(0)
        pendA = []
        for i in range(n_it):
            h, qt, j, njs = iters[i]
            q0 = qt * 256
            gq0 = half * 8 + qt * 2
            c0 = 0 if j <= gq0 else 128
            sbk = 2 * (i % 2)
            Ob = 4 + 2 * ((h * 4 + qt) % 2)
            Lb = Ob + 1
            O3 = ps[:, Ob, :].rearrange("p (a q) -> p a q", a=2)
            L3 = ps[:, Lb, :].rearrange("p (a q) -> p a q", a=2)
            if i + 1 < n_it:
                emit_qk(i + 1)
            S3 = ps[:, sbk:sbk + 2, 0:256]
            pblk = i % 3
            P3 = T[:, pblk, :].rearrange("p (a q) -> p a q", a=2)
            act(P3[:, :, c0:256], S3[:, :, c0:256], AF.Exp, [bankb[sbk], bankb[sbk + 1]], [Tbuf[pblk]])
            if j >= gq0:
                for c in range(2):
                    dve(lambda e: e.tensor_tensor(out=P3[:, c, c0:c0 + 128], in0=P3[:, c, c0:c0 + 128], in1=tri[:], op=ALU.mult),
                        [Tbuf[pblk], constb], [Tbuf[pblk]])
            vj = V[:, j, h * 128:(h + 1) * 128]
            if c0 == 0:
                pv = [(O3[:, :, :], vj, P3[:, :, :], j == 0, j == njs - 1),
                      (L3[:, :, :], ones[:], P3[:, :, :], j == 0, j == njs - 1)]
            else:
                pv = []
                for c in range(2):
                    pv.append((O3[:, c, c0:256], vj, P3[:, c, c0:256], False, c == 1 and j == njs - 1))
                    pv.append((L3[:, c, c0:256], ones[:], P3[:, c, c0:256], False, c == 1 and j == njs - 1))
            mm_raw(pv, [Vb[j // 4], Tbuf[pblk], constb], [bankb[Ob], bankb[Lb]])
            if pending and i - pending[0][0] >= 8:
                pending.pop(0)[1]()
            if pendA and pendA[0][0] < i:
                pendA.pop(0)[1]()
            if j != njs - 1:
                continue

            def stage_a(i=i, h=h, qt=qt, q0=q0, Ob=Ob, Lb=Lb):
              if pending:
                pending.pop(0)[1]()
              Ocp = Tf32(3)
              Lcp = Tf32(5)
              act(Lcp, bank_ap(Lb), AF.Ln, [bankb[Lb]], [Tbuf[5], Tbuf[6]])
              act(Lcp, Lcp, AF.Exp, [Tbuf[5], Tbuf[6]], [Tbuf[5], Tbuf[6]], scale=-1.0)
              dve(lambda e: e.tensor_tensor(out=Ocp, in0=bank_ap(Ob), in1=Lcp, op=ALU.mult),
                  [bankb[Ob], Tbuf[5], Tbuf[6]], [Tbuf[3], Tbuf[4]])
              diff = Lcp[:, 0:256]
              dve(lambda e: e.scalar_tensor_tensor(out=diff, in0=Ocp[:, 256:512], scalar=lamc[:, 0:1], in1=Ocp[:, 0:256],
                                                   op0=ALU.mult, op1=ALU.add),
                  [Tbuf[3], Tbuf[4], lpb], [Tbuf[5]])
              sqd = T[:, 6, 0:256]
              dve(lambda e: e.tensor_tensor(out=sqd, in0=diff, in1=diff, op=ALU.mult), [Tbuf[5]], [Tbuf[6]])

              def stage_b(h=h, qt=qt, q0=q0, diff=diff, sqd=sqd, Ocp=Ocp):
                  ssb = 7 if False else 1
                  mm_raw([(ps[:, ssb, 256:512], ones[:], sqd, True, True)], [Tbuf[6], constb], [bankb[ssb]])
                  rstd = Ocp[:, 256:512]
                  act(rstd, ps[:, ssb, 256:512], AF.Ln, [bankb[ssb]], [Tbuf[4]], scale=1.0 / 128, bias=EPS)
                  act(rstd, rstd, AF.Exp, [Tbuf[4]], [Tbuf[4]], scale=-0.5, bias=math.log(1.0 - li))
                  dve(lambda e: e.scalar_tensor_tensor(out=yT[:, 4 + h, q0:q0 + 256], in0=diff, scalar=cols[:, cb + 36:cb + 37], in1=rstd,
                                                       op0=ALU.mult, op1=ALU.mult),
                      [Tbuf[5], Tbuf[4], constb], [yb[1][qt // 2]])

              pending.append((i, stage_b))
            pendA.append((i, stage_a))
        while pendA:
            pendA.pop(0)[1]()
        attn_tail_flush.append(([], pending))

    def phase_gmlp(l, half):
        gmlp_pass1(l, half)
        while attn_tail_flush:
            pa, pb = attn_tail_flush.pop(0)
            while pa:
                pa.pop(0)[1]()
            while pb:
                pb.pop(0)[1]()
        slot, slb = acquire()
        proj_feat(slot, slb, 4, lambda g, tile: RQ[:, g, tile * 512:(tile + 1) * 512], AF.Gelu, 1.0, lambda g, tile: [RQb])
        release()
        gmlp_pass2(l, half)

    def gmlp_pass1(l, half):
        slot, slb = acquire()
        vall = RMm.rearrange("p c t -> p (c t)").bitcast(F32).rearrange("p (c e) -> p c e", c=8)
        for chunk in range(8):
            b = proj_tok(slot, slb, chunk)
            act(vall[:, chunk, :], bank_ap(b), AF.Gelu, [bankb[b]], [RMb, vallb[chunk]])
            dve(lambda e: e.bn_stats(out=stat2[:, chunk, 0:6], in_=vall[:, chunk, :]), [vallb[chunk]], [statb])
            dve(lambda e: e.bn_aggr(out=stat2[:, chunk, 6:8], in_=stat2[:, chunk, 0:6]), [statb], [statb])
        release()
        act(stat2[:, :, 8:9], stat2[:, :, 7:8], AF.Ln, [statb], [statb], scale=1.0, bias=EPS)
        act(stat2[:, :, 8:9], stat2[:, :, 8:9], AF.Exp, [statb], [statb], scale=-0.5)
        dve(lambda e: e.tensor_tensor(out=stat2[:, :, 9:10], in0=stat2[:, :, 6:7], in1=stat2[:, :, 8:9], op=ALU.mult), [statb], [statb])
        dve(lambda e: e.tensor_scalar(out=stat2[:, :, 9:10], in0=stat2[:, :, 9:10], scalar1=-1.0, scalar2=None, op0=ALU.mult),
            [statb], [statb])
    def gmlp_pass2(l, half):
        vall = RMm.rearrange("p c t -> p (c t)").bitcast(F32).rearrange("p (c e) -> p c e", c=8)
        pslot, pslb = acquire()
        for chunk in range(8):
            vf = vall[:, chunk, :]
            vb_ = [vallb[chunk]]
            dve(lambda e: e.tensor_scalar(out=vf, in0=vf, scalar1=stat2[:, chunk, 8:9], scalar2=stat2[:, chunk, 9:10],
                                          op0=ALU.mult, op1=ALU.add), vb_ + [statb], vb_)
            dve(lambda e: e.tensor_tensor(out=vf, in0=vf, in1=gbc[:, 0, :], op=ALU.mult), vb_ + [lpb], vb_)
            nblk = 4 + (chunk % 2)
            vn = T[:, nblk, :]
            dve(lambda e: e.tensor_tensor(out=vn, in0=vf, in1=gbc[:, 1, :], op=ALU.add), vb_ + [lpb], [Tbuf[nblk]])
            mb = alloc_bank()
            items = []
            for g in range(4):
                o = ps[:, mb, g * 128:(g + 1) * 128]
                items.append((o, vn[:, g * 128:(g + 1) * 128], wsT[:, g, :], True, False))
                items.append((o, rows[0:1, 512:640], rows[0:1, g * 128:(g + 1) * 128], False, True))
            mm_raw(items, [Tbuf[nblk], lpb, constb], [bankb[mb]])
            tile = chunk // 4
            dve(lambda e: e.tensor_tensor(
                out=yT[:, 0:4, chunk * 128:(chunk + 1) * 128], in0=ps[:, mb, :].rearrange("p (g t) -> p g t", g=4),
                in1=RQ[:, :, chunk * 128:(chunk + 1) * 128], op=ALU.mult),
                [bankb[mb], RQb], [yb[0][tile]])
            pbk = proj_tok(pslot, pslb, chunk)
            act(RMp[:, chunk, :], bank_ap(pbk), AF.Copy, [bankb[pbk]], [RMb, vallb[chunk // 2]])
        release()

    def phase_pool(l, half):
        cb = l * NCOL
        for chunk in range(8):
            gchunk = half * 8 + chunk
            b = alloc_bank()
            items = []
            for g in range(4):
                o = ps[:, b, g * 128:(g + 1) * 128]
                cur = RMp[:, chunk, g * 128:(g + 1) * 128]
                if gchunk == 0:
                    items.append((o, cur, pm[:, 8 + g, :], True, True))
                else:
                    prev = carry[:, g * 128:(g + 1) * 128] if chunk == 0 else RMp[:, chunk - 1, g * 128:(g + 1) * 128]
                    items.append((o, cur, pm[:, g, :], True, False))
                    items.append((o, prev, pm[:, 4 + g, :], False, True))
            mm_raw(items, [RMb, carryb, constb], [bankb[b]])
            evac_copy(chunk, RQ[:, :, chunk * 128:(chunk + 1) * 128], ps[:, b, :].rearrange("p (g t) -> p g t", g=4),
                      [bankb[b]], [RQb])
        if half == 0:
            dve(lambda e: e.tensor_copy(out=carry[:], in_=RMp[:, 7, :]), [RMb], [carryb])
        for g in range(4):
            for tile in range(2):
                b = alloc_bank()
                mm_raw([(bank_ap(b), wpl[:, g, :], RQ[:, g, tile * 512:(tile + 1) * 512], True, True)], [lpb, RQb], [bankb[b]])
                act(yT[:, 8 + g, tile * 512:(tile + 1) * 512], bank_ap(b), AF.Copy, [bankb[b], constb], [yb[2][tile]],
                    scale=cols[:, cb + 32 + g:cb + 33 + g])

    def phase_merge(l, half):
        for dc in range(8):
            gsl, gsb = acquire()
            bsl, bsb = acquire()
            gv = gsl[:, 0:3072].rearrange("p (i k c) -> p i k c", i=3, k=8)
            bv = bsl[:, 0:1536].rearrange("p (i k c) -> p i k c", i=3, k=4)
            for tile in range(2):
                tsl = slice(tile * 512, (tile + 1) * 512)
                gb = [Tf32(0), Tf32(2)]
                gbb = [[Tbuf[0], Tbuf[1]], [Tbuf[2], Tbuf[3]]]
                tmp = Tf32(4)
                tmpb = [Tbuf[4], Tbuf[5]]
                m = Tf32(6)
                mbuf = [Tbuf[6], Tbuf[7]]
                Gbs = []
                for i in range(3):
                    Gb = alloc_bank()
                    mm_groups([(bank_ap(Gb), [(gv[:, i, kc, :], hT[:, kc, tsl]) for kc in range(8)])], gsb + [hb[tile]], [bankb[Gb]])
                    Gbs.append(Gb)
                for i in range(3):
                    Gb = Gbs[i]
                    Ub = alloc_bank()
                    mm_groups([(bank_ap(Ub), [(bv[:, i, wc, :], yT[:, i * 4 + wc, tsl]) for wc in range(4)])],
                              bsb + [yb[i][tile]], [bankb[Ub]])
                    k = i % 2
                    act(gb[k], bank_ap(Gb), AF.Sigmoid, [bankb[Gb]], gbb[k])
                    if i == 0:
                        dve(lambda e: e.tensor_tensor(out=m, in0=bank_ap(Ub), in1=gb[k], op=ALU.mult),
                            [bankb[Ub]] + gbb[k], mbuf)
                    else:
                        dve(lambda e: e.tensor_tensor(out=tmp, in0=bank_ap(Ub), in1=gb[k], op=ALU.mult),
                            [bankb[Ub]] + gbb[k], tmpb)
                        if i == 1:
                            dve(lambda e: e.tensor_tensor(out=m, in0=m, in1=tmp, op=ALU.add), mbuf + tmpb, mbuf)
                        else:
                            dve(lambda e: e.tensor_tensor(out=RMm[:, dc, tsl], in0=m, in1=tmp, op=ALU.add),
                                mbuf + tmpb, [RMb] + vallb)
            release()
            release()

    attn_tail_flush = []
    bg = []

    def drain(n=1):
        for _ in range(n):
            if bg:
                bg.pop(0)()

    def resid_update(l, half, tile, src_fn, srcbufs, ssb, colbase, final, tmpB):
        tg = half * 1024 + tile * 512
        t4 = half * 2 + tile
        blk = 4 + 2 * tile
        rs = Tf32(blk)
        rstd_ops(rs, ssb, 512, 1.0 / D, None, [bankb[ssb]], [Tbuf[blk], Tbuf[blk + 1]])

        def unit(oc):
            sa = src_fn(oc)
            dve(lambda e: e.tensor_tensor(out=sa, in0=sa, in1=rs, op=ALU.mult),
                srcbufs(oc) + [Tbuf[blk], Tbuf[blk + 1]], srcbufs(oc))
            dve(lambda e: e.scalar_tensor_tensor(out=xT[:, oc, tg:tg + 512], in0=sa, scalar=cols[:, colbase + oc:colbase + oc + 1],
                                                 in1=xT[:, oc, tg:tg + 512], op0=ALU.mult, op1=ALU.add),
                srcbufs(oc) + [constb], [xb[oc][t4]])

        for oc in range(8):
            bg.append(lambda oc=oc: unit(oc))
        if final:
            def out_unit():
                dst = outd.rearrange("(c p) t -> p c t", p=128)[:, :, tg:tg + 512]
                S.dma("sp", f"OUT{tile}", lambda e: e.dma_start(out=dst, in_=xT[:, :, tg:tg + 512]),
                      reads=[xb[c][t4] for c in range(8)], writes=[outb])
            bg.append(out_unit)
        if IMMEDIATE_RESID:
            drain(10 ** 6)

    def phase_wout(l, half):
        cb = l * NCOL
        sA, sAb = acquire()
        sB, sBb = acquire()
        vA = sA.rearrange("p (k c) -> p k c", k=8)
        vB = sB.rearrange("p (k c) -> p k c", k=8)
        ybufs = [yb[0][0], yb[0][1], yb[1][0], yb[1][1]]
        for tile in range(2):
            tsl = slice(tile * 512, (tile + 1) * 512)
            ssb = alloc_bank(pin=True)
            for oc in range(8):
                wv = vA if oc < 4 else vB
                wb = sAb if oc < 4 else sBb
                b = alloc_bank()
                mm_groups([(bank_ap(b), [(wv[:, kc, (oc % 4) * 128:(oc % 4 + 1) * 128], RMm[:, kc, tsl]) for kc in range(8)])],
                          wb + [RMb], [bankb[b]])
                if oc > 0:
                    pb = (oc - 1) % 2
                    mm_raw([(bank_ap(ssb), ones[:], T[:, pb, :], oc - 1 == 0, False)], [Tbuf[pb], constb], [bankb[ssb]])
                act(oT[:, oc, :], bank_ap(b), AF.Copy, [bankb[b]], [oTb[oc]] + ybufs)
                sblk = oc % 2
                act(T[:, sblk, :], bank_ap(b), AF.Square, [bankb[b]], [Tbuf[sblk]])
            mm_raw([(bank_ap(ssb), ones[:], T[:, 7 % 2, :], False, True)], [Tbuf[7 % 2], constb], [bankb[ssb]])
            resid_update(l, half, tile, lambda oc: oT[:, oc, :], lambda oc: [oTb[oc]], ssb, cb + 8, False, 6 if tile == 0 else 4)
            if tile == 0:
                drain(10 ** 6)
            unpin(ssb)
        release()
        release()

    def phase_ffn_in(l, half):
        it = 0
        for jp in range(11):
            if jp == 4:
                drain(10 ** 6)
            slot, slb = acquire()
            sv = slot.rearrange("p (k a c) -> p k a c", k=8, a=2)
            for jj in range(2):
                j = 2 * jp + jj
                for tile in range(2):
                    tsl = slice(tile * 512, (tile + 1) * 512)
                    Gb = alloc_bank()
                    mm_groups([(bank_ap(Gb), [(sv[:, kc, 0, jj * 128:(jj + 1) * 128], hT[:, kc, tsl]) for kc in range(8)])],
                              slb + [hb[tile]], [bankb[Gb]])
                    Ub = alloc_bank()
                    mm_groups([(bank_ap(Ub), [(sv[:, kc, 1, jj * 128:(jj + 1) * 128], hT[:, kc, tsl]) for kc in range(8)])],
                              slb + [hb[tile]], [bankb[Ub]])
                    k = 2 * (it % 2)
                    it += 1
                    sg = Tf32(k)
                    sgb = [Tbuf[k], Tbuf[k + 1]]
                    act(sg, bank_ap(Gb), AF.Silu, [bankb[Gb]], sgb)
                    dve(lambda e, Ub=Ub, sg=sg, j=j, tsl=tsl: e.tensor_tensor(out=aT[:, j, tsl], in0=bank_ap(Ub), in1=sg, op=ALU.mult),
                        [bankb[Ub]] + sgb, [aTb[tile]])
                    drain(1)
            release()
    def phase_ffn_out(l, half, final):
        cb = l * NCOL
        ssb = [alloc_bank(pin=True), alloc_bank(pin=True)]
        prev_ss = None
        for oc in range(8):
            slot, slb = acquire()
            sv = slot[:, 0:2816].rearrange("p (k c) -> p k c", k=22)
            for tile in range(2):
                tsl = slice(tile * 512, (tile + 1) * 512)
                b = alloc_bank()
                mm_groups([(bank_ap(b), [(sv[:, jc, :], aT[:, jc, tsl]) for jc in range(22)])], slb + [aTb[tile]], [bankb[b]])
                if prev_ss is not None:
                    po, pt, pblk_ = prev_ss
                    mm_raw([(bank_ap(ssb[pt]), ones[:], T[:, pblk_, :], po == 0, po == 7)], [Tbuf[pblk_], constb], [bankb[ssb[pt]]])
                act(fT[:, oc, tsl], bank_ap(b), AF.Copy, [bankb[b]], [fTb[tile][oc]])
                sblk = (2 * oc + tile) % 2
                act(T[:, sblk, :], bank_ap(b), AF.Square, [bankb[b]], [Tbuf[sblk]])
                prev_ss = (oc, tile, sblk)
            release()
        po, pt, pblk_ = prev_ss
        mm_raw([(bank_ap(ssb[pt]), ones[:], T[:, pblk_, :], po == 0, po == 7)], [Tbuf[pblk_], constb], [bankb[ssb[pt]]])
        for tile in range(2):
            tsl = slice(tile * 512, (tile + 1) * 512)
            resid_update(l, half, tile, lambda oc, tsl=tsl: fT[:, oc, tsl], lambda oc, tile=tile: [fTb[tile][oc]], ssb[tile], cb + 24, final, 0)
            unpin(ssb[tile])

    for li_, l in enumerate(layers):
        S.barrier()
        layer_setup(l)
        cb = l * NCOL
        final = li_ == len(layers) - 1
        if li_ == 0:
            phase_norm(l, 0, cb + 0)
        for half in range(2):
            phase_attn(l, half)
            phase_gmlp(l, half)
            phase_pool(l, half)
            phase_merge(l, half)
            if half == 0:
                phase_norm(l, 1, cb + 0)
            else:
                phase_norm(l, 0, cb + 16)
            phase_wout(l, half)
        S.barrier()
        phase_ffn_in(l, 0)
        phase_norm(l, 1, cb + 16)
        phase_ffn_out(l, 0, final)
        phase_ffn_in(l, 1)
        if not final:
            nl = layers[li_ + 1]
            phase_norm(nl, 0, nl * NCOL + 0)
        phase_ffn_out(l, 1, final)
    drain(10 ** 6)
    assert wstate["acq"] == len(loads) and wstate["released"] == len(loads), wstate
    S.q["sp"].append(([(k, S.dmacnt[k]) for k in ("OUT0", "OUT1")], None, None, 0))

    engmap = {"pe": "tensor", "act": "scalar", "dve": "vector", "pool": "gpsimd", "sp": "sync"}
    with nc.Block() as block:
        for en in Sched.ENG:
            def body(e, en=en):
                for waits, fn, key, inc in S.q[en]:
                    for k, v in waits:
                        e.wait_ge(sems[k], v)
                    if fn is None:
                        continue
                    r = fn(e)
                    r.then_inc(sems[key], inc)
            getattr(block, engmap[en])(body)
    es.close()
    return nc


def _consts():
    k = np.arange(128)
    tri = (k[:, None] <= k[None, :]).astype(np.float32)
    pm = np.zeros((128, 12, 128), np.float32)
    s = k[:, None]
    t = k[None, :]
    for g, w in enumerate(C_WINDOWS):
        band = ((s <= t) & (s > t - w)).astype(np.float32)
        eye = (s == t).astype(np.float32)
        pm[:, g, :] = band / w - eye
        pm[:, 4 + g, :] = (s > t + 128 - w).astype(np.float32) / w
        cnt = np.minimum(t + 1, w).astype(np.float32)
        pm[:, 8 + g, :] = band / cnt - eye
    return tri, pm.reshape(128, 12 * 128)


def _pack(inputs):
    f = lambda a: np.ascontiguousarray(np.asarray(a, dtype=np.float32))
    L = 2
    cols = np.zeros((128, L * NCOL), np.float32)
    for l in range(L):
        cb = l * NCOL
        cols[:, cb + 0:cb + 8] = f(inputs["norm_mix_pre"])[l].reshape(8, 128).T
        cols[:, cb + 8:cb + 16] = f(inputs["norm_mix_post"])[l].reshape(8, 128).T
        cols[:, cb + 16:cb + 24] = f(inputs["norm_ffn_pre"])[l].reshape(8, 128).T
        cols[:, cb + 24:cb + 32] = f(inputs["norm_ffn_post"])[l].reshape(8, 128).T
        cols[:, cb + 32:cb + 36] = f(inputs["pool_scale"])[l].reshape(4, 128).T
        cols[:, cb + 36] = f(inputs["diff_subln_g"])[l]
    bc = np.concatenate([f(inputs["gmlp_norm_g"]), f(inputs["gmlp_norm_b"])], axis=1)
    lam = np.concatenate([f(inputs["lambda_q1"]), f(inputs["lambda_k1"]), f(inputs["lambda_q2"]), f(inputs["lambda_k2"])], axis=1)
    wsT = f(np.transpose(f(inputs["gmlp_w_s"]), (0, 3, 1, 2)).reshape(L, 128, 512))
    bs = f(f(inputs["gmlp_b_s"]).reshape(L, 512))
    wp = f(np.transpose(f(inputs["pool_w"]), (0, 2, 1, 3)).reshape(L, 128, 512))
    tri, pm = _consts()
    shared = {
        "w_in": f(inputs["w_in"]), "w_branch": f(inputs["w_branch"]), "w_out": f(inputs["w_out"]),
        "w_ffn_in": f(inputs["w_ffn_in"]), "w_ffn_out": f(inputs["w_ffn_out"]),
        "cols": cols, "bc": f(bc), "lam": f(lam), "wsT": wsT, "bs": bs, "wp": wp, "tri": tri, "pm": pm,
    }
    return shared


N_LAUNCH_LAYERS = [[0, 1]]


def kernel(**inputs):
    x = np.asarray(inputs["x"], dtype=np.float32)
    shared = _pack(inputs)
    cur = [np.ascontiguousarray(x[b].T) for b in range(8)]
    for layers in N_LAUNCH_LAYERS:
        nc = build_program(layers, True, True)
        in_maps = [dict(shared, xT=cur[b]) for b in range(8)]
        res = run_bass_kernel_spmd(nc, in_maps, core_ids=list(range(8)))
        cur = [np.ascontiguousarray(res.results[b]["outT"]) for b in range(8)]
    out = np.stack([c.T for c in cur], axis=0)
    return np.ascontiguousarray(out.astype(np.float32))
```
